# Optimizing a Trainium2 kernel written in Bass

```python
import math
import jax, jax.numpy as jnp
from jax import lax
import numpy as np

D_MODEL = 2048
BATCH = 4
SEQ = 2048
DEPTH = 2
DEC_BATCH = 128
DEC_SEQ = 4
PAST_LEN = 16384
PAGE_SIZE = 128

N_EVEN = (DEPTH + 1) // 2
N_ODD = DEPTH // 2
PLE_DIM = 256
D_FF = 4 * D_MODEL
CONV_W = 4
CHUNK = 64
LN_EPS = 1e-5
RMS_EPS = 1e-6
DN_ALPHA = (2 * DEPTH) ** 0.25
DN_BETA = (8 * DEPTH) ** -0.25

A_HEADS = D_MODEL // 256
A_DK = 128
A_DV = 128
A_QK = A_HEADS * A_DK
A_V = A_HEADS * A_DV
LRU_W = D_MODEL // 2
B_BLOCKS = 8
B_BW = LRU_W // B_BLOCKS
LRU_C = 8.0
CONV_CH = 2 * A_QK + A_V + LRU_W
PROJ_E = CONV_CH + A_V + LRU_W + 2 * A_HEADS
MIX_E = A_V + LRU_W
C_HEADS = D_MODEL // 256
C_DK = 128
C_DV = 256
C_QK = C_HEADS * C_DK
C_V = C_HEADS * C_DV
PROJ_O = 2 * C_QK + 2 * C_V + 2 * C_HEADS

kernel_name = "deltanet_rglru_mlstm_hybrid_step"

F32 = jnp.float32


def layer_norm(x, g, b):
    xf = x.astype(F32)
    mu = jnp.mean(xf, -1, keepdims=True)
    var = jnp.mean(jnp.square(xf - mu), -1, keepdims=True)
    return ((xf - mu) * lax.rsqrt(var + LN_EPS) * g.astype(F32) + b.astype(F32)).astype(x.dtype)


def rms_norm(x, w):
    xf = x.astype(F32)
    return xf * lax.rsqrt(jnp.mean(xf * xf, -1, keepdims=True) + RMS_EPS) * w.astype(F32)


def l2_normalize(x):
    xf = x.astype(F32)
    return xf * lax.rsqrt(jnp.sum(xf * xf, -1, keepdims=True) + 1e-6)


def causal_conv(u, buf, w, b):
    T = u.shape[1]
    full = jnp.concatenate([buf.astype(u.dtype), u], axis=1)
    out = sum(full[:, j:j + T] * w[j] for j in range(CONV_W)) + b
    return out, full[:, T:]


def to_chunks(x, L):
    B, T, H = x.shape[:3]
    x = x.reshape((B, T // L, L, H) + x.shape[3:])
    return jnp.moveaxis(x, (1, 3), (0, 2))


def from_chunks(x):
    x = jnp.moveaxis(x, (0, 2), (1, 3))
    B, NC, L, H = x.shape[:4]
    return x.reshape((B, NC * L, H) + x.shape[4:])


def gated_delta_rule(q, k, v, g, beta, s0, L):
    qc, kc, vc = (to_chunks(t.astype(F32), L) for t in (q, k, v))
    gc = jnp.cumsum(to_chunks(g.astype(F32), L), axis=-1)
    bc = to_chunks(beta.astype(F32), L)
    idx = jnp.arange(L)
    tril = idx[:, None] >= idx[None, :]
    decay = jnp.exp(jnp.where(tril, gc[..., :, None] - gc[..., None, :], -jnp.inf))
    kb = kc * bc[..., None]
    m_strict = jnp.where(idx[:, None] > idx[None, :],
                         jnp.einsum('nbhid,nbhjd->nbhij', kb, kc) * decay, 0.0)
    eye = jnp.eye(L, dtype=F32)
    t_inv = lax.linalg.triangular_solve(eye + m_strict, jnp.broadcast_to(eye, m_strict.shape),
                                        left_side=True, lower=True)
    u = jnp.einsum('nbhij,nbhjd->nbhid', t_inv, vc * bc[..., None])
    w = jnp.einsum('nbhij,nbhjd->nbhid', t_inv, kb * jnp.exp(gc)[..., None])
    qk = jnp.einsum('nbhid,nbhjd->nbhij', qc, kc) * decay

    def step(s, xs):
        q_i, k_i, u_i, w_i, qk_i, g_i = xs
        v_new = u_i - jnp.einsum('bhik,bhkv->bhiv', w_i, s)
        o = (jnp.einsum('bhik,bhkv->bhiv', q_i * jnp.exp(g_i)[..., None], s)
             + jnp.einsum('bhij,bhjv->bhiv', qk_i, v_new))
        g_last = g_i[..., -1:]
        s = (s * jnp.exp(g_last)[..., None]
             + jnp.einsum('bhik,bhiv->bhkv', k_i * jnp.exp(g_last - g_i)[..., None], v_new))
        return s, o

    s_fin, o = lax.scan(step, s0.astype(F32), (qc, kc, u, w, qk, gc))
    return from_chunks(o), s_fin


def rg_lru(x, r_pre, i_pre, lam, h0):
    log_a = -LRU_C * jax.nn.sigmoid(r_pre.astype(F32)) * jax.nn.softplus(-lam.astype(F32))
    a = jnp.exp(log_a)
    bx = jnp.sqrt(-jnp.expm1(2.0 * log_a)) * jax.nn.sigmoid(i_pre.astype(F32)) * x.astype(F32)
    bx = bx.at[:, 0].add(a[:, 0] * h0.astype(F32))

    def combine(left, right):
        a1, b1 = left
        a2, b2 = right
        return a1 * a2, a2 * b1 + b2

    _, h = lax.associative_scan(combine, (a, bx), axis=1)
    return h, h[:, -1]


def mlstm_chunked(q, k, v, ig, fg, c0, n0, m0, L):
    qc, kc, vc = (to_chunks(t.astype(F32), L) for t in (q, k, v))
    igc = to_chunks(ig.astype(F32), L)
    bcum = jnp.cumsum(jax.nn.log_sigmoid(to_chunks(fg.astype(F32), L)), axis=-1)
    idx = jnp.arange(L)
    tril = idx[:, None] >= idx[None, :]
    d_intra = jnp.where(tril, bcum[..., :, None] - bcum[..., None, :] + igc[..., None, :], -jnp.inf)
    g_end = bcum[..., -1:] - bcum + igc
    qk = jnp.einsum('nbhtk,nbhsk->nbhts', qc, kc)

    def step(carry, xs):
        c, n, m = carry
        q_i, k_i, v_i, b_i, d_i, ge_i, qk_i = xs
        inter = b_i + m[..., None]
        m_t = jnp.maximum(inter, jnp.max(d_i, -1))
        e = jnp.exp(inter - m_t)
        s = qk_i * jnp.exp(d_i - m_t[..., None])
        num = (e[..., None] * jnp.einsum('bhtk,bhvk->bhtv', q_i, c)
               + jnp.einsum('bhts,bhsv->bhtv', s, v_i))
        den = e * jnp.einsum('bhtk,bhk->bht', q_i, n) + jnp.sum(s, -1)
        h = num / jnp.maximum(jnp.abs(den), jnp.exp(-m_t))[..., None]
        b_last = b_i[..., -1]
        m_new = jnp.maximum(b_last + m, jnp.max(ge_i, -1))
        sc = jnp.exp(b_last + m - m_new)
        w_s = jnp.exp(ge_i - m_new[..., None])
        c = sc[..., None, None] * c + jnp.einsum('bhs,bhsv,bhsk->bhvk', w_s, v_i, k_i)
        n = sc[..., None] * n + jnp.einsum('bhs,bhsk->bhk', w_s, k_i)
        return (c, n, m_new), h

    (c, n, m), h = lax.scan(step, (c0.astype(F32), n0.astype(F32), m0.astype(F32)),
                            (qc, kc, vc, bcum, d_intra, g_end, qk))
    return from_chunks(h), c, n, m


def delta_lru_mixer(x, conv_buf, s_delta, h_lru, w_in, w_conv, b_conv, a_log, dt_bias, norm_w,
                    w_r, b_r, w_i, b_i, lam, w_out, L):
    B, T, _ = x.shape
    proj = x @ w_in
    s1 = CONV_CH
    s2 = s1 + A_V
    s3 = s2 + LRU_W
    s4 = s3 + A_HEADS
    conv_in, z, gate, a_pre, b_pre = jnp.split(proj, [s1, s2, s3, s4], axis=-1)
    conv_out, conv_new = causal_conv(conv_in, conv_buf, w_conv, b_conv)
    q, k, v, xr = jnp.split(conv_out, [A_QK, 2 * A_QK, 2 * A_QK + A_V], axis=-1)
    q = l2_normalize(jax.nn.silu(q).reshape(B, T, A_HEADS, A_DK)) * (A_DK ** -0.5)
    k = l2_normalize(jax.nn.silu(k).reshape(B, T, A_HEADS, A_DK))
    v = jax.nn.silu(v).reshape(B, T, A_HEADS, A_DV)
    g = -jnp.exp(a_log.astype(F32)) * jax.nn.softplus(a_pre.astype(F32) + dt_bias.astype(F32))
    beta = jax.nn.sigmoid(b_pre.astype(F32))
    o, s_new = gated_delta_rule(q, k, v, g, beta, s_delta, L)
    y_a = (rms_norm(o, norm_w) * jax.nn.silu(z.reshape(B, T, A_HEADS, A_DV).astype(F32))).reshape(B, T, A_V)
    xb = xr.reshape(B, T, B_BLOCKS, B_BW)
    r_pre = jnp.einsum('btnc,ncd->btnd', xb, w_r).reshape(B, T, LRU_W) + b_r
    i_pre = jnp.einsum('btnc,ncd->btnd', xb, w_i).reshape(B, T, LRU_W) + b_i
    h, h_last = rg_lru(xr, r_pre, i_pre, lam, h_lru)
    y_b = h * jax.nn.gelu(gate.astype(F32))
    y = jnp.concatenate([y_a, y_b], axis=-1).astype(x.dtype) @ w_out
    return y, conv_new, s_new, h_last


def mlstm_mixer(x, c0, n0, m0, w_in, b_ig, b_fg, norm_w, w_out, L):
    B, T, _ = x.shape
    proj = x @ w_in
    cuts = [C_QK, 2 * C_QK, 2 * C_QK + C_V, 2 * C_QK + 2 * C_V, 2 * C_QK + 2 * C_V + C_HEADS]
    q, k, v, o_pre, ig, fg = jnp.split(proj, cuts, axis=-1)
    q = q.reshape(B, T, C_HEADS, C_DK)
    k = k.reshape(B, T, C_HEADS, C_DK) * (C_DK ** -0.5)
    v = v.reshape(B, T, C_HEADS, C_DV)
    ig = ig.astype(F32) + b_ig.astype(F32)
    fg = fg.astype(F32) + b_fg.astype(F32)
    h, c, n, m = mlstm_chunked(q, k, v, ig, fg, c0, n0, m0, L)
    y = rms_norm(h, norm_w.reshape(C_HEADS, C_DV)) * jax.nn.sigmoid(o_pre.reshape(B, T, C_HEADS, C_DV).astype(F32))
    y = y.reshape(B, T, C_V).astype(x.dtype) @ w_out
    return y, c, n, m


def run_trunk(x, p, conv, delta, lru, mc, mn, mm, W):
    T = x.shape[1]
    L = math.gcd(T, CHUNK)
    conv_o, delta_o, lru_o, mc_o, mn_o, mm_o = [], [], [], [], [], []
    for layer in range(DEPTH):
        j = layer // 2
        if layer % 2 == 0:
            mix, cb, sd, hl = delta_lru_mixer(
                x, conv[j], delta[j], lru[j], W['w_in_e'][j], W['w_conv_e'][j], W['b_conv_e'][j],
                W['a_log_e'][j], W['dt_bias_e'][j], W['delta_norm_e'][j], W['lru_wr_e'][j],
                W['lru_br_e'][j], W['lru_wi_e'][j], W['lru_bi_e'][j], W['lru_lambda_e'][j],
                W['w_out_e'][j], L)
            conv_o.append(cb)
            delta_o.append(sd)
            lru_o.append(hl)
        else:
            mix, c, n, m = mlstm_mixer(
                x, mc[j], mn[j], mm[j], W['w_in_o'][j], W['b_ig_o'][j], W['b_fg_o'][j],
                W['mlstm_norm_o'][j], W['w_out_o'][j], L)
            mc_o.append(c)
            mn_o.append(n)
            mm_o.append(m)
        h = layer_norm(DN_ALPHA * x + mix.astype(x.dtype), W['ln1_g'][layer], W['ln1_b'][layer])
        ff = jnp.square(jax.nn.relu(h @ W['w_up'][layer])) @ W['w_down'][layer]
        h = layer_norm(DN_ALPHA * h + ff, W['ln2_g'][layer], W['ln2_b'][layer])
        gate = jax.nn.sigmoid((h @ W['w_ple_gate'][layer]).astype(F32))
        x = (h.astype(F32) + gate * (p[layer] @ W['w_ple'][layer]).astype(F32)).astype(x.dtype)
    return (x, jnp.stack(conv_o), jnp.stack(delta_o), jnp.stack(lru_o),
            jnp.stack(mc_o), jnp.stack(mn_o), jnp.stack(mm_o))


def setup_inputs(seed: int = 0) -> dict:
    key = jax.random.key(seed)
    ks = iter(jax.random.split(key, 48))

    def nrm(shape, scale):
        return jax.random.normal(next(ks), shape, F32) * scale

    def unif(shape, lo, hi):
        return jax.random.uniform(next(ks), shape, F32, lo, hi)

    d = {}
    d['x_prompt'] = nrm((BATCH, SEQ, D_MODEL), 1.0)
    d['x_sample'] = nrm((DEC_BATCH, DEC_SEQ, D_MODEL), 1.0)
    d['p_prompt'] = nrm((DEPTH, BATCH, SEQ, PLE_DIM), 1.0)
    d['p_sample'] = nrm((DEPTH, DEC_BATCH, DEC_SEQ, PLE_DIM), 1.0)
    d['state_conv'] = nrm((N_EVEN, DEC_BATCH, CONV_W - 1, CONV_CH), 1.0)
    d['state_delta'] = nrm((N_EVEN, DEC_BATCH, A_HEADS, A_DK, A_DV), 0.3)
    d['state_lru'] = nrm((N_EVEN, DEC_BATCH, LRU_W), 0.5)
    d['state_mlstm_c'] = nrm((N_ODD, DEC_BATCH, C_HEADS, C_DV, C_DK), 0.3)
    d['state_mlstm_n'] = nrm((N_ODD, DEC_BATCH, C_HEADS, C_DK), 0.3)
    d['state_mlstm_m'] = unif((N_ODD, DEC_BATCH, C_HEADS), 0.0, 2.0)
    d['w_in_e'] = nrm((N_EVEN, D_MODEL, PROJ_E), D_MODEL ** -0.5)
    d['w_conv_e'] = nrm((N_EVEN, CONV_W, CONV_CH), CONV_W ** -0.5)
    d['b_conv_e'] = nrm((N_EVEN, CONV_CH), 0.01)
    d['a_log_e'] = jnp.log(unif((N_EVEN, A_HEADS), 1.0, 16.0))
    dt = jnp.exp(unif((N_EVEN, A_HEADS), math.log(1e-3), math.log(1e-1)))
    d['dt_bias_e'] = dt + jnp.log(-jnp.expm1(-dt))
    d['delta_norm_e'] = 1.0 + nrm((N_EVEN, A_DV), 0.02)
    d['lru_wr_e'] = nrm((N_EVEN, B_BLOCKS, B_BW, B_BW), B_BW ** -0.5)
    d['lru_br_e'] = nrm((N_EVEN, LRU_W), 0.01)
    d['lru_wi_e'] = nrm((N_EVEN, B_BLOCKS, B_BW, B_BW), B_BW ** -0.5)
    d['lru_bi_e'] = nrm((N_EVEN, LRU_W), 0.01)
    u = unif((N_EVEN, LRU_W), 0.9, 0.999) ** (1.0 / LRU_C)
    d['lru_lambda_e'] = jnp.log(u) - jnp.log1p(-u)
    d['w_out_e'] = nrm((N_EVEN, MIX_E, D_MODEL), MIX_E ** -0.5 * DN_BETA)
    d['w_in_o'] = nrm((N_ODD, D_MODEL, PROJ_O), D_MODEL ** -0.5)
    d['b_ig_o'] = nrm((N_ODD, C_HEADS), 0.1)
    d['b_fg_o'] = jnp.linspace(3.0, 6.0, C_HEADS, dtype=F32)[None, :] + nrm((N_ODD, C_HEADS), 0.1)
    d['mlstm_norm_o'] = 1.0 + nrm((N_ODD, C_V), 0.02)
    d['w_out_o'] = nrm((N_ODD, C_V, D_MODEL), C_V ** -0.5 * DN_BETA)
    d['ln1_g'] = 1.0 + nrm((DEPTH, D_MODEL), 0.02)
    d['ln1_b'] = nrm((DEPTH, D_MODEL), 0.02)
    d['ln2_g'] = 1.0 + nrm((DEPTH, D_MODEL), 0.02)
    d['ln2_b'] = nrm((DEPTH, D_MODEL), 0.02)
    d['w_up'] = nrm((DEPTH, D_MODEL, D_FF), D_MODEL ** -0.5)
    d['w_down'] = nrm((DEPTH, D_FF, D_MODEL), D_FF ** -0.5 * DN_BETA)
    d['w_ple'] = nrm((DEPTH, PLE_DIM, D_MODEL), PLE_DIM ** -0.5 * 0.5)
    d['w_ple_gate'] = nrm((DEPTH, D_MODEL, D_MODEL), D_MODEL ** -0.5)
    return d


def reference(x_prompt, x_sample, p_prompt, p_sample, state_conv, state_delta, state_lru,
              state_mlstm_c, state_mlstm_n, state_mlstm_m, w_in_e, w_conv_e, b_conv_e, a_log_e,
              dt_bias_e, delta_norm_e, lru_wr_e, lru_br_e, lru_wi_e, lru_bi_e, lru_lambda_e, w_out_e,
              w_in_o, b_ig_o, b_fg_o, mlstm_norm_o, w_out_o, ln1_g, ln1_b, ln2_g, ln2_b, w_up, w_down,
              w_ple, w_ple_gate):
    W = dict(w_in_e=w_in_e, w_conv_e=w_conv_e, b_conv_e=b_conv_e, a_log_e=a_log_e,
             dt_bias_e=dt_bias_e, delta_norm_e=delta_norm_e, lru_wr_e=lru_wr_e, lru_br_e=lru_br_e,
             lru_wi_e=lru_wi_e, lru_bi_e=lru_bi_e, lru_lambda_e=lru_lambda_e, w_out_e=w_out_e,
             w_in_o=w_in_o, b_ig_o=b_ig_o, b_fg_o=b_fg_o, mlstm_norm_o=mlstm_norm_o, w_out_o=w_out_o,
             ln1_g=ln1_g, ln1_b=ln1_b, ln2_g=ln2_g, ln2_b=ln2_b, w_up=w_up, w_down=w_down,
             w_ple=w_ple, w_ple_gate=w_ple_gate)
    bp = x_prompt.shape[0]
    z_conv = jnp.zeros((N_EVEN, bp, CONV_W - 1, CONV_CH), x_prompt.dtype)
    z_delta = jnp.zeros((N_EVEN, bp, A_HEADS, A_DK, A_DV), F32)
    z_lru = jnp.zeros((N_EVEN, bp, LRU_W), F32)
    z_c = jnp.zeros((N_ODD, bp, C_HEADS, C_DV, C_DK), F32)
    z_n = jnp.zeros((N_ODD, bp, C_HEADS, C_DK), F32)
    z_m = jnp.zeros((N_ODD, bp, C_HEADS), F32)
    y_prompt, conv_p, delta_p, lru_p, mc_p, mn_p, mm_p = run_trunk(
        x_prompt, p_prompt, z_conv, z_delta, z_lru, z_c, z_n, z_m, W)
    y_sample, conv_s, delta_s, lru_s, mc_s, mn_s, mm_s = run_trunk(
        x_sample, p_sample, state_conv, state_delta, state_lru,
        state_mlstm_c, state_mlstm_n, state_mlstm_m, W)
    return (y_prompt, y_sample, conv_p, delta_p, lru_p, mc_p, mn_p, mm_p,
            conv_s, delta_s, lru_s, mc_s, mn_s, mm_s)
```

```python
import numpy as np
from contextlib import ExitStack
import concourse.bass as bass
import concourse.mybir as mybir
from concourse.bass_utils import run_bass_kernel_spmd

F32 = mybir.dt.float32
BF16 = mybir.dt.bfloat16
AF = mybir.ActivationFunctionType
ALU = mybir.AluOpType
AX = mybir.AxisListType

D = 2048
KC = 16
L = 64
NTP = 512
WT = 256
NTS = 64
NPASS_P = 4
ALPHA = 4.0 ** 0.25
LN_EPS = 1e-5
RMS_EPS = 1e-6
NEG = -1.0e30

C_ID = 0
C_P = 128
C_S = 384
C_RM = 640
C_SM = 656
C_NTP = 656 + 1024
C_NTS = C_NTP + 64
C_END = C_NTS + 64


class V:
    __slots__ = ("ap", "keys")

    def __init__(self, ap, keys=()):
        self.ap = ap
        self.keys = tuple(keys)

    def __getitem__(self, idx):
        return V(self.ap[idx], self.keys)

    def re(self, pat, **kw):
        return V(self.ap.rearrange(pat, **kw), self.keys)

    def bc(self, shape):
        return V(self.ap.to_broadcast(shape), self.keys)


class TL:
    def __init__(self, t, name, nslots):
        self.t = t
        self.name = name
        self.n = nslots

    def __getitem__(self, idx):
        if not isinstance(idx, tuple):
            idx = (idx,)
        if len(idx) < 2:
            ks = range(self.n)
        else:
            s = idx[1]
            if isinstance(s, int):
                ks = [s]
            else:
                ks = range(*s.indices(self.n))
        return V(self.t[idx], [(self.name, i) for i in ks])


def asv(x):
    return x if isinstance(x, V) else V(x, ())


class K:
    def __init__(self, nc, es):
        self.nc = nc
        self.es = es
        self.eng = dict(pe=nc.tensor, act=nc.scalar, dve=nc.vector, pool=nc.gpsimd, sp=nc.sync)
        self.sem = {e: es.enter_context(nc.semaphore("s_" + e)) for e in self.eng}
        self.cnt = {e: 0 for e in self.eng}
        self.known = {e: {} for e in self.eng}
        self.lastw = {}
        self.readers = {}
        self.dsem = {}
        for q, n in (("sp", 20), ("act", 6), ("pool", 6)):
            self.dsem[q] = [[es.enter_context(nc.semaphore(f"d_{q}{i}")), 0] for i in range(n)]
        self.dnext = {q: 0 for q in self.dsem}
        self.psb = []
        for i in range(8):
            t = es.enter_context(nc.psum_tensor(f"psb{i}", [128, 512], F32))
            self.psb.append(V(t[:], [("psb", i)]))
        self.psn = 0
        self.ntile = 0
        self.log = {e: [] for e in self.eng}

    def tile(self, name, shape, dt=F32, slots=0):
        t = self.es.enter_context(self.nc.sbuf_tensor(name, list(shape), dt))
        if slots:
            return TL(t, name, slots)
        return V(t[:], [(name, 0)])

    def ps(self):
        v = self.psb[self.psn % 8]
        self.psn += 1
        return v

    def _deps(self, outs, ins):
        deps = {}

        def add(tok):
            sem, val = tok
            o = deps.get(sem.name)
            if o is None or o[1] < val:
                deps[sem.name] = (sem, val)

        for v in ins:
            for kk in v.keys:
                w = self.lastw.get(kk)
                if w:
                    add(w)
                if kk[0] == "psb":
                    for r in self.readers.get(kk, {}).values():
                        add(r)
        for v in outs:
            for kk in v.keys:
                w = self.lastw.get(kk)
                if w:
                    add(w)
                for r in self.readers.get(kk, {}).values():
                    add(r)
        return deps

    def _wait(self, e, deps):
        kn = self.known[e]
        for name, (sem, val) in deps.items():
            if e == "pe" and name == "s_pe":
                continue
            if kn.get(name, 0) >= val:
                continue
            self.eng[e].wait_ge(sem, val)
            self.log[e].append(("w", name, val))
            kn[name] = val

    def _record(self, tok, outs, ins):
        sem = tok[0]
        for v in outs:
            for kk in v.keys:
                self.lastw[kk] = tok
                self.readers[kk] = {}
        for v in ins:
            for kk in v.keys:
                self.readers.setdefault(kk, {})[sem.name] = tok

    def op(self, e, fn, outs, ins, sig=True):
        outs = [asv(o) for o in outs]
        ins = [asv(i) for i in ins if isinstance(i, V)]
        self._wait(e, self._deps(outs, ins))
        inst = fn(self.eng[e])
        if sig:
            inst.then_inc(self.sem[e], 1)
            self.log[e].append(("i", "s_" + e, 1))
            self.cnt[e] += 1
            tok = (self.sem[e], self.cnt[e])
        else:
            tok = (self.sem[e], self.cnt[e] + 1)
        self._record(tok, outs, ins)

    def dma(self, q, out, in_):
        out = asv(out)
        in_ = asv(in_)
        deps = self._deps([out], [in_])
        pool = self.dsem[q]
        slot = pool[self.dnext[q] % len(pool)]
        self.dnext[q] += 1
        sem, tot = slot
        if tot:
            deps[sem.name] = (sem, tot)
        self._wait(q, deps)
        self.eng[q].dma_start(out=out.ap, in_=in_.ap).then_inc(sem, 16)
        self.log[q].append(("i", sem.name, 16))
        slot[1] = tot + 16
        self._record((sem, tot + 16), [out], [in_])

    def finish(self):
        deps = {}
        for q in self.dsem:
            for sem, tot in self.dsem[q]:
                if tot:
                    deps[sem.name] = (sem, tot)
        for e in ("pe", "act", "dve"):
            if self.cnt[e]:
                deps["s_" + e] = (self.sem[e], self.cnt[e])
        self._wait("sp", dict(deps))

    def mm(self, out, lhsT, rhs, start=True, stop=True, sig=None):
        if sig is None:
            sig = True
        self.op("pe", lambda e: e.matmul(out.ap, lhsT=lhsT.ap, rhs=rhs.ap, start=start, stop=stop),
                [out], [lhsT, rhs], sig=sig)

    def act(self, out, in_, func, bias=None, scale=1.0, accum=None):
        ins = [in_]
        kw = {}
        if bias is not None:
            kw["bias"] = bias.ap if isinstance(bias, V) else bias
            if isinstance(bias, V):
                ins.append(bias)
        if isinstance(scale, V):
            ins.append(scale)
            kw["scale"] = scale.ap
        else:
            kw["scale"] = scale
        outs = [out]
        if accum is not None:
            kw["accum_out"] = accum.ap
            outs.append(accum)
        self.op("act", lambda e: e.activation(out=out.ap, in_=in_.ap, func=func, **kw), outs, ins)

    def tt(self, out, in0, in1, op, e="dve"):
        self.op(e, lambda g: g.tensor_tensor(out=out.ap, in0=in0.ap, in1=in1.ap, op=op), [out], [in0, in1])

    def ts(self, out, in0, s1, s2=None, op0=ALU.mult, op1=None, e="dve"):
        ins = [in0]
        a1 = s1.ap if isinstance(s1, V) else s1
        a2 = s2.ap if isinstance(s2, V) else s2
        if isinstance(s1, V):
            ins.append(s1)
        if isinstance(s2, V):
            ins.append(s2)
        if op1 is None:
            self.op(e, lambda g: g.tensor_scalar(out=out.ap, in0=in0.ap, scalar1=a1, scalar2=None, op0=op0), [out], ins)
        else:
            self.op(e, lambda g: g.tensor_scalar(out=out.ap, in0=in0.ap, scalar1=a1, scalar2=a2, op0=op0, op1=op1), [out], ins)

    def stt(self, out, in0, scalar, in1, op0, op1):
        ins = [in0, in1]
        sc = scalar.ap if isinstance(scalar, V) else scalar
        if isinstance(scalar, V):
            ins.append(scalar)
        self.op("dve", lambda g: g.scalar_tensor_tensor(out=out.ap, in0=in0.ap, scalar=sc, in1=in1.ap, op0=op0, op1=op1),
                [out], ins)

    def copy(self, out, in_, e="dve"):
        if e == "act":
            self.op("act", lambda g: g.activation(out=out.ap, in_=in_.ap, func=AF.Copy), [out], [in_])
        else:
            self.op(e, lambda g: g.tensor_copy(out=out.ap, in_=in_.ap), [out], [in_])

    def memset(self, out, val, e="dve"):
        self.op(e, lambda g: g.memset(out.ap, val), [out], [])

    def recip(self, out, in_):
        self.op("dve", lambda g: g.reciprocal(out=out.ap, in_=in_.ap), [out], [in_])

    def reduce(self, out, in_, op):
        self.op("dve", lambda g: g.tensor_reduce(out=out.ap, in_=in_.ap, axis=AX.X, op=op), [out], [in_])

    def scan(self, out, d0, d1, init):
        self.op("dve", lambda g: g.tensor_tensor_scan(out=out.ap, data0=d0.ap, data1=d1.ap, initial=init.ap,
                                                       op0=ALU.mult, op1=ALU.add), [out], [d0, d1, init])


class StopBuild(Exception):
    pass


class Prog:
    def stop(self, stage):
        import os
        if os.environ.get("KSTOP") == stage:
            raise StopBuild()

    def __init__(self):
        nc = bass.Bass("TRN2", target_bir_lowering=False)
        self.nc = nc
        di = lambda n, s: nc.dram_tensor(n, list(s), F32, kind="ExternalInput").ap()
        do = lambda n, s: nc.dram_tensor(n, list(s), F32, kind="ExternalOutput").ap()
        I = self.I = {}
        for n, s in IN_SPECS:
            I[n] = di(n, s)
        O = self.O = {}
        for n, s in OUT_SPECS:
            O[n] = do(n, s)
        with ExitStack() as es:
            self.k = K(nc, es)
            self.alloc()
            self.setup()
            try:
                self.stop("setup")
                for ps in range(NPASS_P + 1):
                    self.run_pass(ps)
            except StopBuild:
                self.store_prompt_states()
            self.k.finish()

    def alloc(self):
        k = self.k
        self.xres = k.tile("xres", [128, KC, NTP], F32, slots=KC)
        self.xb = k.tile("xb", [128, KC, NTP], BF16, slots=KC)
        self.mixb = k.tile("mixb", [128, KC, NTP], BF16, slots=KC)
        self.wbuf = [k.tile(f"wbuf{i}", [128, KC * WT], BF16) for i in range(3)]
        self.wn = 0
        self.wpb = [k.tile(f"wpb{i}", [128, 2 * WT], BF16) for i in range(2)]
        self.wpn = 0
        self.pT = k.tile("pT", [128, 2, NTP], BF16)
        self.xrb = k.tile("xrb", [128, 2, NTP], BF16)
        self.cst = k.tile("cst", [128, C_END], F32)
        self.ones = k.tile("ones", [128, 128], F32)
        gg = k.es.enter_context(k.nc.sbuf_tensor("GG", [128, 4 * 1024], F32))
        self.GG = gg
        self.G = [V(gg[:, i * 1024:(i + 1) * 1024], [("G", i)]) for i in range(4)]
        self.U = [k.tile(f"U{i}", [128, 520], F32) for i in range(2)]
        self.un = 0
        self.scr = [k.tile(f"scr{i}", [128, 512], F32) for i in range(6)]
        self.sn = 0
        self.msk = k.tile("msk", [128, 1024], F32)
        self.cw = k.tile("cw", [128, 128], F32)
        self.lnp = k.tile("lnp", [128, 128], F32)
        self.pp3 = k.tile("pp3", [128, 128], F32)
        self.nsl = k.tile("nsl", [128, 16], F32)
        self.hp = k.tile("hp", [128, 32], F32)
        self.wrb = k.tile("wrb", [128, 8 * 128], BF16)
        self.wib = k.tile("wib", [128, 8 * 128], BF16)
        self.carry_p = k.tile("carry_p", [128, 32, 3], F32, slots=32)
        self.carry_s = k.tile("carry_s", [128, 32, 48], F32, slots=32)
        self.S_p = k.tile("S_p", [128, 8, 128], F32, slots=8)
        self.Cn_p = k.tile("Cn_p", [128, 8, 257], F32, slots=8)
        self.hl = k.tile("hl", [128, 8, 16], F32, slots=8)
        self.mst = k.tile("mst", [128, 128], F32)
        self.iobuf = [k.tile(f"iobuf{i}", [128, 1024], F32) for i in range(2)]
        self.ion = 0
        self.t_ab = k.tile("t_ab", [128, 128], F32)
        self.t_g = k.tile("t_g", [128, 64], F32)
        self.t_b = k.tile("t_b", [128, 64], F32)
        self.t_gc = k.tile("t_gc", [128, 64], F32)
        self.t_egc = k.tile("t_egc", [128, 64], F32)
        self.t_edec = k.tile("t_edec", [128, 64], F32)
        self.t_gend = k.tile("t_gend", [128, 64], F32)
        self.t_egl = k.tile("t_egl", [128, 128], F32)
        self.ct = {n: k.tile("ct_" + n, [128, 256], F32) for n in
                   ("egrow", "dmd", "bm", "qkm", "n0", "n0t", "kg", "kdec", "vtm", "vnew", "wnt", "qgt",
                    "osig", "hr", "yt", "rn")}
        self.vaug_t = k.tile("vaug_t", [128, 260], F32)
        self.nd_t = k.tile("nd_t", [128, 260], F32)
        self.vw_t = k.tile("vw_t", [128, 260], F32)
        self.npool = [k.tile(f"np{i}", [128, 128], F32) for i in range(12)]
        self.nn = 0
        self.small = [k.tile(f"sm{i}", [128, 128], F32) for i in range(10)]
        self.smn = 0

    def sc(self):
        v = self.scr[self.sn % len(self.scr)]
        self.sn += 1
        return v

    def npl(self):
        v = self.npool[self.nn % len(self.npool)]
        self.nn += 1
        return v

    def sm(self):
        v = self.small[self.smn % len(self.small)]
        self.smn += 1
        return v

    def io(self):
        v = self.iobuf[self.ion % 2]
        self.ion += 1
        return v

    def ident(self, n=128):
        return self.cst[0:n, C_ID:C_ID + n]

    def trans(self, out_ps, in_):
        P = in_.ap.shape[0]
        self.k.mm(out_ps, in_, self.ident(P), True, True)

    def wload(self, view, kcn, ncols):
        buf = self.wbuf[self.wn % 3]
        self.wn += 1
        dst = V(buf.ap[:, 0:kcn * ncols].rearrange("p (k n) -> p k n", k=kcn), buf.keys)
        self.k.dma("pool", dst, view)
        return dst

    def setup(self):
        k = self.k
        I = self.I
        k.dma("sp", self.cst, I["consts"])
        k.memset(self.ones, 1.0)
        k.memset(self.vaug_t, 1.0)
        st = self.io()
        r128 = lambda ap: ap.rearrange("j (c p) -> (j c) p", p=128)
        k.dma("sp", st[:, 0:128], r128(I["w_conv_e"]))
        k.dma("sp", st[0:32, 128:256], r128(I["ln1_g"]))
        k.dma("sp", st[32:64, 128:256], r128(I["ln1_b"]))
        k.dma("sp", st[64:96, 128:256], r128(I["ln2_g"]))
        k.dma("sp", st[96:128, 128:256], r128(I["ln2_b"]))
        k.memset(st[:, 256:384], 0.0)
        k.dma("sp", st[0:32, 256:384], r128(I["b_conv_e"]))
        k.dma("sp", st[32:40, 256:384], r128(I["lru_br_e"]))
        k.dma("sp", st[40:48, 256:384], r128(I["lru_bi_e"]))
        k.dma("sp", st[48:56, 256:384], r128(I["lru_lambda_e"]))
        k.dma("sp", st[56:57, 256:384], I["delta_norm_e"])
        k.dma("sp", st[57:73, 256:384], r128(I["mlstm_norm_o"]))
        for i, dst in enumerate((self.cw, self.lnp, self.pp3)):
            ps = k.ps()
            self.trans(ps[:, 0:128], st[:, i * 128:(i + 1) * 128])
            k.copy(dst, ps[:, 0:128])
        t = self.sm()
        k.act(t[:, 0:8], self.pp3[:, 48:56], AF.Exp, scale=-1.0)
        k.act(t[:, 0:8], t[:, 0:8], AF.Ln, bias=1.0)
        k.ts(self.nsl[:, 0:8], t[:, 0:8], -8.0)
        k.ts(self.nsl[:, 8:16], t[:, 0:8], -16.0)
        k.dma("sp", self.hp[0:64, 0:8], I["a_log_e"].to_broadcast([64, 8]))
        k.dma("sp", self.hp[0:64, 8:16], I["dt_bias_e"].to_broadcast([64, 8]))
        k.dma("sp", self.hp[0:64, 16:24], I["b_ig_o"].to_broadcast([64, 8]))
        k.dma("sp", self.hp[0:64, 24:32], I["b_fg_o"].to_broadcast([64, 8]))
        k.act(self.hp[0:64, 0:8], self.hp[0:64, 0:8], AF.Exp)
        k.ts(self.hp[0:64, 0:8], self.hp[0:64, 0:8], -1.0)
        k.dma("pool", self.wrb.re("p (n d) -> p n d", n=8), I["lru_wr_e"].rearrange("n c d -> c n d"))
        k.dma("pool", self.wib.re("p (n d) -> p n d", n=8), I["lru_wi_e"].rearrange("n c d -> c n d"))
        k.memset(self.carry_p[:, :, :], 0.0)
        k.memset(self.S_p[:, :, :], 0.0)
        k.memset(self.Cn_p[:, :, :], 0.0)
        k.memset(self.hl[:, :, :], 0.0)
        k.memset(self.mst, 0.0)

    def run_pass(self, ps):
        k = self.k
        I = self.I
        O = self.O
        smp = ps == NPASS_P
        NT = NTS if smp else NTP
        self.NT = NT
        self.smp = smp
        self.nseq = 16 if smp else 1
        self.T = 4 if smp else NTP
        self.nch = NT // L
        mc = C_S if smp else C_P
        c64 = lambda o: self.cst[0:64, mc + o:mc + o + 64]
        self.MUd, self.MU, self.SLm, self.NEGm = c64(0), c64(64), c64(128), c64(192)
        self.RM = self.cst[0:64, C_RM:C_RM + 16]
        nt0 = C_NTS if smp else C_NTP
        self.NEGmT = self.cst[0:64, nt0:nt0 + 64]
        self.SM3 = self.cst[:, C_SM:C_SM + 1024].re("p (s i) -> p s i", s=16)
        self.I64 = self.ident(64)
        xsrc = I["x_s"] if smp else I["x_p"][ps * NTP:(ps + 1) * NTP, :]
        ydst = O["y_s"] if smp else O["y_p"][ps * NTP:(ps + 1) * NTP, :]
        ntt = (NT + 127) // 128
        tsz = min(NT, 128)
        if smp:
            self.load_sample_states()
        import os
        kx = int(os.environ.get("KX", "99"))
        kxm = int(os.environ.get("KXM", "1"))
        kxc = int(os.environ.get("KXC", "1"))
        for tt in range(ntt):
            for hf in range(2):
                if tt * 2 + hf >= kx:
                    continue
                b = self.io()
                k.dma("sp", b[0:tsz, :], xsrc[tt * 128:tt * 128 + tsz, hf * 1024:(hf + 1) * 1024])
                for c4 in range(2):
                    if not kxm:
                        continue
                    pst = k.ps()
                    for j in range(4):
                        self.trans(pst[:, j * 128:j * 128 + tsz], b[0:tsz, (c4 * 4 + j) * 128:(c4 * 4 + j + 1) * 128])
                    for j in range(4):
                        if not kxc:
                            continue
                        c = hf * 8 + c4 * 4 + j
                        k.copy(self.xres[:, c, tt * 128:tt * 128 + tsz], pst[:, j * 128:j * 128 + tsz], e="act")
                        k.copy(self.xb[:, c, tt * 128:tt * 128 + tsz], self.xres[:, c, tt * 128:tt * 128 + tsz], e="act")
        if ps == 0:
            self.stop("x")
        for layer in range(2):
            if layer == 0:
                self.mixer_e()
                wo = I["w_out_e"]
                if ps == 0:
                    self.stop("e")
            else:
                self.mixer_o()
                wo = I["w_out_o"]
                if ps == 0:
                    self.stop("o")
            wv = wo.rearrange("(kc p) n -> p kc n", p=128)

            def ev_res(c, pst):
                k.stt(self.xres[:, c, 0:NT], self.xres[:, c, 0:NT], ALPHA, pst, ALU.mult, ALU.add)
            self.dense(wv, KC, D, self.mixb, ev_res)
            if ps == 0:
                self.stop("res%d" % layer)
            self.layer_norm(0 * 32 + layer * 16, 32 + layer * 16)
            if ps == 0:
                self.stop("ln1_%d" % layer)
            self.mlp(layer)
            if ps == 0:
                self.stop("mlp%d" % layer)
            self.layer_norm(64 + layer * 16, 96 + layer * 16)
            if ps == 0:
                self.stop("ln2_%d" % layer)
            self.ple(layer, ps)
            if ps == 0:
                self.stop("l%d" % layer)
        for tt in range(ntt):
            for hf in range(2):
                b = self.io()
                for c4 in range(2):
                    pst = k.ps()
                    for j in range(4):
                        c = hf * 8 + c4 * 4 + j
                        self.trans(pst[0:tsz, j * 128:(j + 1) * 128], self.xres[:, c, tt * 128:tt * 128 + tsz])
                    k.copy(b[0:tsz, c4 * 512:(c4 + 1) * 512], pst[0:tsz, :], e="act" if c4 % 2 else "dve")
                k.dma("sp", ydst[tt * 128:tt * 128 + tsz, hf * 1024:(hf + 1) * 1024], b[0:tsz, :])
        if ps == NPASS_P - 1:
            self.store_prompt_states()
        if smp:
            self.store_sample_states()

    def dense(self, wv, kcn, ncols, xin, evac):
        k = self.k
        NT = self.NT
        for t0 in range(0, ncols, WT):
            n_ = min(WT, ncols - t0)
            wt = self.wload(wv[:, :, t0:t0 + n_], kcn, n_)
            for cc in range(n_ // 128):
                pst = k.ps()
                for kc in range(kcn):
                    k.mm(pst[:, 0:NT], wt[:, kc, cc * 128:(cc + 1) * 128], xin[:, kc, 0:NT], kc == 0, kc == kcn - 1, sig=(kc == kcn - 1))
                evac(t0 // 128 + cc, pst[:, 0:NT])

    def layer_norm(self, goff, boff):
        k = self.k
        NT = self.NT
        prm = self.lnp
        p1 = k.ps()
        p2 = k.ps()
        for c in range(KC):
            sq = self.sc()
            k.act(sq[:, 0:NT], self.xres[:, c, 0:NT], AF.Square)
            k.mm(p1[:, 0:NT], self.ones, self.xres[:, c, 0:NT], c == 0, c == KC - 1)
            k.mm(p2[:, 0:NT], self.ones, sq[:, 0:NT], c == 0, c == KC - 1)
        mean, rstd, var = self.G[0], self.G[1], self.G[2]
        k.ts(mean[:, 0:NT], p1[:, 0:NT], 1.0 / D)
        k.tt(var[:, 0:NT], mean[:, 0:NT], mean[:, 0:NT], ALU.mult)
        k.stt(var[:, 0:NT], p2[:, 0:NT], 1.0 / D, var[:, 0:NT], ALU.mult, ALU.subtract)
        k.act(rstd[:, 0:NT], var[:, 0:NT], AF.Sqrt, bias=LN_EPS)
        k.recip(rstd[:, 0:NT], rstd[:, 0:NT])
        k.tt(mean[:, 0:NT], mean[:, 0:NT], rstd[:, 0:NT], ALU.mult)
        for c in range(KC):
            t = self.sc()
            k.tt(t[:, 0:NT], self.xres[:, c, 0:NT], rstd[:, 0:NT], ALU.mult)
            k.tt(t[:, 0:NT], t[:, 0:NT], mean[:, 0:NT], ALU.subtract)
            k.act(self.xres[:, c, 0:NT], t[:, 0:NT], AF.Identity, bias=prm[:, boff + c:boff + c + 1],
                  scale=prm[:, goff + c:goff + c + 1])
            k.copy(self.xb[:, c, 0:NT], self.xres[:, c, 0:NT], e="act")

    def mlp(self, layer):
        k = self.k
        NT = self.NT
        I = self.I
        wu = I["w_up"][layer].rearrange("(kc p) n -> p kc n", p=128)
        wd = I["w_down"][layer].rearrange("(kc p) n -> p kc n", p=128)
        hb = self.mixb
        for g in range(4):
            def ev_up(c, pst):
                sq = self.sc()
                k.act(sq[:, 0:NT], pst, AF.Square)
                k.stt(hb[:, c, 0:NT], pst, 0.0, sq[:, 0:NT], ALU.is_gt, ALU.mult)
            self.dense(wu[:, :, g * 2048:(g + 1) * 2048], KC, 2048, self.xb, ev_up)

            def ev_dn(c, pst, g=g):
                if g == 0:
                    k.stt(self.xres[:, c, 0:NT], self.xres[:, c, 0:NT], ALPHA, pst, ALU.mult, ALU.add)
                else:
                    k.tt(self.xres[:, c, 0:NT], self.xres[:, c, 0:NT], pst, ALU.add)
            self.dense(wd[:, g * KC:(g + 1) * KC, :], KC, D, hb, ev_dn)

    def ple(self, layer, ps):
        k = self.k
        NT = self.NT
        I = self.I
        psrc = I["p_s"][layer] if self.smp else I["p_p"][layer][ps * NTP:(ps + 1) * NTP, :]
        ntt = (NT + 127) // 128
        tsz = min(NT, 128)
        pT = self.pT
        for tt in range(ntt):
            b = self.io()
            k.dma("sp", b[0:tsz, 0:256], psrc[tt * 128:tt * 128 + tsz, :])
            pst = k.ps()
            for j in range(2):
                self.trans(pst[:, j * 128:j * 128 + tsz], b[0:tsz, j * 128:(j + 1) * 128])
            for j in range(2):
                k.copy(pT[:, j, tt * 128:tt * 128 + tsz], pst[:, j * 128:j * 128 + tsz])
        wpl = I["w_ple"][layer].rearrange("(kc p) n -> p kc n", p=128)
        wg = I["w_ple_gate"][layer].rearrange("(kc p) n -> p kc n", p=128)
        nxb = self.mixb
        for t0 in range(0, D, WT):
            wt = self.wload(wg[:, :, t0:t0 + WT], KC, WT)
            wp = self.wpb[self.wpn % 2].re("p (k n) -> p k n", k=2)
            self.wpn += 1
            k.dma("pool", wp, wpl[:, :, t0:t0 + WT])
            for cc in range(WT // 128):
                c = t0 // 128 + cc
                pst = k.ps()
                for kc in range(KC):
                    k.mm(pst[:, 0:NT], wt[:, kc, cc * 128:(cc + 1) * 128], self.xb[:, kc, 0:NT], kc == 0, kc == KC - 1, sig=(kc == KC - 1))
                gt = self.sc()
                k.act(gt[:, 0:NT], pst[:, 0:NT], AF.Sigmoid)
                p2 = k.ps()
                for j in range(2):
                    k.mm(p2[:, 0:NT], wp[:, j, cc * 128:(cc + 1) * 128], pT[:, j, 0:NT], j == 0, j == 1)
                k.tt(gt[:, 0:NT], gt[:, 0:NT], p2[:, 0:NT], ALU.mult)
                k.tt(self.xres[:, c, 0:NT], self.xres[:, c, 0:NT], gt[:, 0:NT], ALU.add)
                k.copy(nxb[:, c, 0:NT], self.xres[:, c, 0:NT], e="act")
        self.xb, self.mixb = self.mixb, self.xb

    def conv_chunk(self, pst, ch, dest, silu):
        k = self.k
        NT, nseq, T = self.NT, self.nseq, self.T
        U = self.U[self.un % 2]
        self.un += 1
        U3 = U[:, 0:nseq * (T + 3)].re("p (s t) -> p s t", s=nseq)
        if self.smp:
            car = self.carry_s[:, ch, :].re("p (s r) -> p s r", r=3)
        else:
            car = self.carry_p[:, ch, :].re("p (s r) -> p s r", s=1)
        k.copy(U3[:, :, 3:T + 3], pst.re("p (s t) -> p s t", s=nseq), e="act")
        k.copy(U3[:, :, 0:3], car)
        acc = self.sc()
        a3 = acc[:, 0:NT].re("p (s t) -> p s t", s=nseq)
        cwc = lambda j: self.cw[:, j * 32 + ch:j * 32 + ch + 1]
        k.act(a3, U3[:, :, 0:T], AF.Identity, bias=self.pp3[:, ch:ch + 1], scale=cwc(0))
        for j in range(1, 4):
            k.stt(a3, U3[:, :, j:j + T], cwc(j), a3, ALU.mult, ALU.add)
        k.copy(car, U3[:, :, T:T + 3])
        if silu:
            k.act(dest, acc[:, 0:NT], AF.Silu)
        else:
            k.copy(dest, acc[:, 0:NT], e="act")

    def l2norm_fm(self, t, scale):
        k = self.k
        NT = self.NT
        sq = self.sc()
        k.act(sq[:, 0:NT], t, AF.Square)
        pss = k.ps()
        k.mm(pss[:, 0:NT], self.ones, sq[:, 0:NT])
        k.act(sq[:, 0:NT], pss[:, 0:NT], AF.Sqrt, bias=1e-6)
        k.recip(sq[:, 0:NT], sq[:, 0:NT])
        k.stt(t, t, scale, sq[:, 0:NT], ALU.mult, ALU.mult)

    def tok_scalars(self, wv):
        k = self.k
        NT, nch = self.NT, self.nch
        wt = self.wload(wv[:, :, 6144:6160], KC, 16)
        pst = k.ps()
        for kc in range(KC):
            k.mm(pst[0:16, 0:NT], wt[:, kc, 0:16], self.xb[:, kc, 0:NT], kc == 0, kc == KC - 1, sig=(kc == KC - 1))
        abT = self.sc()
        k.copy(abT[0:16, 0:NT], pst[0:16, 0:NT])
        pst2 = k.ps()
        for c in range(nch):
            self.trans(pst2[0:64, c * 16:(c + 1) * 16], abT[0:16, c * 64:(c + 1) * 64])
        k.copy(self.t_ab[0:64, 0:nch * 16], pst2[0:64, 0:nch * 16])
        return self.t_ab[0:64, 0:nch * 16].re("p (c f) -> p c f", f=16)

    def seq_rep(self, src2d, dst, func=None):
        k = self.k
        nch = self.nch
        pst = k.ps()
        if self.smp:
            g2 = self.sm()
            k.tt(g2[0:64, 0:128].re("p (s h) -> p s h", h=8), src2d[:, None, :].bc([64, 16, 8]),
                 self.RM[:, :, None].bc([64, 16, 8]), ALU.mult)
            k.mm(pst[:, 0:128], self.ones[0:64, :], g2[0:64, 0:128])
            n = 128
        else:
            k.mm(pst[:, 0:nch * 8], self.ones[0:64, :], src2d)
            n = nch * 8
        if func is None:
            k.copy(dst[:, 0:n], pst[:, 0:n])
        else:
            k.act(dst[:, 0:n], pst[:, 0:n], func)

    def mixer_e(self):
        k = self.k
        I = self.I
        NT, nch = self.NT, self.nch
        wv = I["w_in_e"].rearrange("(kc p) n -> p kc n", p=128)
        ab3 = self.tok_scalars(wv)
        nb = nch * 8
        g3 = self.t_g[0:64, 0:nb].re("p (c h) -> p c h", h=8)
        b3 = self.t_b[0:64, 0:nb].re("p (c h) -> p c h", h=8)
        z = self.sm()
        z3 = z[0:64, 0:nb].re("p (c h) -> p c h", h=8)
        k.tt(z3, ab3[:, :, 0:8], self.hp[0:64, None, 8:16].bc([64, nch, 8]), ALU.add)
        k.act(z3, z3, AF.Exp)
        k.act(z3, z3, AF.Ln, bias=1.0)
        k.tt(g3, z3, self.hp[0:64, None, 0:8].bc([64, nch, 8]), ALU.mult)
        k.act(b3, ab3[:, :, 8:16], AF.Sigmoid)
        g2 = self.t_g[0:64, 0:nb]
        pst = k.ps()
        k.mm(pst[0:64, 0:nb], self.MUd, g2)
        k.copy(self.t_gc[0:64, 0:nb], pst[0:64, 0:nb])
        k.act(self.t_egc[0:64, 0:nb], pst[0:64, 0:nb], AF.Exp)
        pst = k.ps()
        k.mm(pst[0:64, 0:nb], self.SLm, g2)
        tmp = self.sm()
        k.tt(tmp[0:64, 0:nb], pst[0:64, 0:nb], self.t_gc[0:64, 0:nb], ALU.subtract)
        k.act(self.t_edec[0:64, 0:nb], tmp[0:64, 0:nb], AF.Exp)
        self.seq_rep(g2, self.t_egl, AF.Exp)
        self.stop("ts")
        for gi in range(4):
            qT, kT, vT, oT = [self.G[i][:, 0:2 * NT].re("p (j n) -> p j n", j=2) for i in range(4)]
            for (col0, dst, chb, kind) in ((gi * 256, qT, gi * 2, "q"), (1024 + gi * 256, kT, 8 + gi * 2, "k"),
                                          (2048 + gi * 256, vT, 16 + gi * 2, "v")):
                wt = self.wload(wv[:, :, col0:col0 + 256], KC, 256)
                for cc in range(2):
                    pst = k.ps()
                    for kc in range(KC):
                        k.mm(pst[:, 0:NT], wt[:, kc, cc * 128:(cc + 1) * 128], self.xb[:, kc, 0:NT], kc == 0, kc == KC - 1, sig=(kc == KC - 1))
                    self.conv_chunk(pst[:, 0:NT], chb + cc, dst[:, cc, :], True)
                    if kind == "q":
                        self.l2norm_fm(dst[:, cc, :], 128.0 ** -0.5)
                    elif kind == "k":
                        self.l2norm_fm(dst[:, cc, :], 1.0)
            self.stop("g0")
            for c in range(nch):
                self.delta_chunk(gi, c, qT, kT, vT, oT)
                self.stop("c0")
            wt = self.wload(wv[:, :, 4096 + gi * 256:4096 + gi * 256 + 256], KC, 256)
            for j in range(2):
                h = gi * 2 + j
                pst = k.ps()
                for kc in range(KC):
                    k.mm(pst[:, 0:NT], wt[:, kc, j * 128:(j + 1) * 128], self.xb[:, kc, 0:NT], kc == 0, kc == KC - 1, sig=(kc == KC - 1))
                zs = self.sc()
                k.act(zs[:, 0:NT], pst[:, 0:NT], AF.Silu)
                sq = self.sc()
                k.act(sq[:, 0:NT], oT[:, j, :], AF.Square)
                pss = k.ps()
                k.mm(pss[:, 0:NT], self.ones, sq[:, 0:NT])
                k.act(sq[:, 0:NT], pss[:, 0:NT], AF.Sqrt, bias=RMS_EPS, scale=1.0 / 128)
                k.recip(sq[:, 0:NT], sq[:, 0:NT])
                k.tt(sq[:, 0:NT], sq[:, 0:NT], oT[:, j, :], ALU.mult)
                k.stt(self.mixb[:, h, 0:NT], sq[:, 0:NT], self.pp3[:, 56:57], zs[:, 0:NT], ALU.mult, ALU.mult)
        for li in range(4):
            xr, gg = [self.G[i][:, 0:2 * NT].re("p (j n) -> p j n", j=2) for i in range(2)]
            wt = self.wload(wv[:, :, 3072 + li * 256:3072 + li * 256 + 256], KC, 256)
            for cc in range(2):
                pst = k.ps()
                for kc in range(KC):
                    k.mm(pst[:, 0:NT], wt[:, kc, cc * 128:(cc + 1) * 128], self.xb[:, kc, 0:NT], kc == 0, kc == KC - 1, sig=(kc == KC - 1))
                self.conv_chunk(pst[:, 0:NT], 24 + li * 2 + cc, xr[:, cc, :], False)
                k.copy(self.xrb[:, cc, 0:NT], xr[:, cc, :], e="act")
            wt = self.wload(wv[:, :, 5120 + li * 256:5120 + li * 256 + 256], KC, 256)
            for cc in range(2):
                pst = k.ps()
                for kc in range(KC):
                    k.mm(pst[:, 0:NT], wt[:, kc, cc * 128:(cc + 1) * 128], self.xb[:, kc, 0:NT], kc == 0, kc == KC - 1, sig=(kc == KC - 1))
                x0 = self.sc()
                k.copy(x0[:, 0:NT], pst[:, 0:NT], e="act")
                t = self.sc()
                k.tt(t[:, 0:NT], x0[:, 0:NT], x0[:, 0:NT], ALU.mult)
                k.ts(t[:, 0:NT], t[:, 0:NT], 0.044715, 1.0, ALU.mult, ALU.add)
                k.tt(t[:, 0:NT], t[:, 0:NT], x0[:, 0:NT], ALU.mult)
                k.act(t[:, 0:NT], t[:, 0:NT], AF.Sigmoid, scale=1.5957691216057308)
                k.tt(gg[:, cc, :], t[:, 0:NT], x0[:, 0:NT], ALU.mult)
            for cc in range(2):
                blk = li * 2 + cc
                pr = k.ps()
                k.mm(pr[:, 0:NT], self.wrb[:, blk * 128:(blk + 1) * 128], self.xrb[:, cc, 0:NT])
                ra = self.sc()
                k.act(ra[:, 0:NT], pr[:, 0:NT], AF.Sigmoid, bias=self.pp3[:, 32 + blk:33 + blk])
                pi = k.ps()
                k.mm(pi[:, 0:NT], self.wib[:, blk * 128:(blk + 1) * 128], self.xrb[:, cc, 0:NT])
                ia = self.sc()
                k.act(ia[:, 0:NT], pi[:, 0:NT], AF.Sigmoid, bias=self.pp3[:, 40 + blk:41 + blk])
                a = self.sc()
                k.act(a[:, 0:NT], ra[:, 0:NT], AF.Exp, scale=self.nsl[:, blk:blk + 1])
                bx = self.sc()
                k.act(bx[:, 0:NT], ra[:, 0:NT], AF.Exp, scale=self.nsl[:, 8 + blk:9 + blk])
                k.ts(bx[:, 0:NT], bx[:, 0:NT], -1.0, 1.0, ALU.mult, ALU.add)
                k.act(bx[:, 0:NT], bx[:, 0:NT], AF.Sqrt)
                k.tt(bx[:, 0:NT], bx[:, 0:NT], ia[:, 0:NT], ALU.mult)
                k.tt(bx[:, 0:NT], bx[:, 0:NT], xr[:, cc, :], ALU.mult)
                hT = self.sc()
                if not self.smp:
                    k.scan(hT[:, 0:NT], a[:, 0:NT], bx[:, 0:NT], self.hl[:, blk, 0:1])
                    k.copy(self.hl[:, blk, 0:1], hT[:, NT - 1:NT])
                else:
                    h3 = hT[:, 0:NT].re("p (s t) -> p s t", t=4)
                    a3 = a[:, 0:NT].re("p (s t) -> p s t", t=4)
                    b3_ = bx[:, 0:NT].re("p (s t) -> p s t", t=4)
                    for t_ in range(4):
                        prev = self.hl[:, blk, :] if t_ == 0 else h3[:, :, t_ - 1]
                        k.tt(h3[:, :, t_], a3[:, :, t_], prev, ALU.mult)
                        k.tt(h3[:, :, t_], h3[:, :, t_], b3_[:, :, t_], ALU.add)
                    k.copy(self.hl[:, blk, :], h3[:, :, 3])
                k.tt(self.mixb[:, 8 + blk, 0:NT], hT[:, 0:NT], gg[:, cc, :], ALU.mult)

    def delta_chunk(self, gi, c, qT, kT, vT, oT):
        k = self.k
        nseq = self.nseq
        ct = self.ct
        cs = slice(c * 64, (c + 1) * 64)
        n0 = c * 8 + gi * 2
        qc, kc_, vc = qT[:, :, cs], kT[:, :, cs], vT[:, :, cs]
        bc3 = lambda t, w: t[0:64, n0:n0 + 2][:, :, None].bc([64, 2, w])
        m3 = lambda m, w=64: m[:, None, :].bc([64, 2, w])
        v3 = lambda t, p=64, w=64: t[0:p, 0:2 * w].re("p (j i) -> p j i", j=2)
        dg = self.sm()
        k.tt(v3(dg), m3(self.MUd), bc3(self.t_g, 64), ALU.mult)
        pgr = k.ps()
        k.mm(pgr[:, 0:128], self.ones[0:64, :], dg[0:64, 0:128])
        k.act(ct["egrow"][:, 0:128], pgr[:, 0:128], AF.Exp)
        a1 = self.sm()
        k.tt(v3(a1), pgr[0:64, 0:128].re("p (j i) -> p j i", j=2), bc3(self.t_gc, 64), ALU.subtract)
        k.tt(v3(a1), v3(a1), m3(self.NEGm), ALU.add)
        k.act(ct["dmd"][0:64, 0:128], a1[0:64, 0:128], AF.Exp)
        k.tt(v3(ct["bm"]), m3(self.MU), bc3(self.t_b, 64), ALU.mult)
        pkk = k.ps()
        pqk = k.ps()
        for j in range(2):
            k.mm(pkk[0:64, j * 64:(j + 1) * 64], kc_[:, j, :], kc_[:, j, :], sig=(j == 1))
        for j in range(2):
            k.mm(pqk[0:64, j * 64:(j + 1) * 64], kc_[:, j, :], qc[:, j, :], sig=(j == 1))
        N0 = ct["n0"]
        k.tt(N0[0:64, 0:128], pkk[0:64, 0:128], ct["dmd"][0:64, 0:128], ALU.mult)
        k.tt(N0[0:64, 0:128], N0[0:64, 0:128], ct["bm"][0:64, 0:128], ALU.mult)
        k.tt(ct["qkm"][0:64, 0:128], pqk[0:64, 0:128], ct["dmd"][0:64, 0:128], ALU.mult)
        pnt = k.ps()
        for j in range(2):
            self.k.mm(pnt[0:64, j * 64:(j + 1) * 64], N0[0:64, j * 64:(j + 1) * 64], self.I64, sig=(j == 1))
        k.copy(ct["n0t"][0:64, 0:128], pnt[0:64, 0:128], e="act")
        X = self.npl()
        k.stt(v3(X), v3(N0), -1.0, m3(self.I64), ALU.mult, ALU.add)
        A, AT = N0, ct["n0t"]
        js = lambda t, j: t[0:64, j * 64:(j + 1) * 64]
        for lvl in range(1, 6):
            if lvl < 5:
                pa = k.ps()
                for j in range(2):
                    k.mm(js(pa, j), js(AT, j), js(A, j), sig=(j == 1))
            pat = k.ps()
            for j in range(2):
                k.mm(js(pat, j), js(A, j), js(AT, j), sig=(j == 1))
            ATn = self.npl()
            BT = self.npl()
            k.copy(ATn[0:64, 0:128], pat[0:64, 0:128], e="act")
            k.tt(v3(BT), pat[0:64, 0:128].re("p (j i) -> p j i", j=2), m3(self.I64), ALU.add)
            if lvl < 5:
                An = self.npl()
                k.copy(An[0:64, 0:128], pa[0:64, 0:128])
            px = k.ps()
            for j in range(2):
                k.mm(js(px, j), js(BT, j), js(X, j), sig=(j == 1))
            Xn = self.npl()
            k.copy(Xn[0:64, 0:128], px[0:64, 0:128], e="act")
            if lvl < 5:
                A = An
            AT, X = ATn, Xn
        YT = X
        pkt = k.ps()
        for j in range(2):
            k.mm(pkt[0:64, j * 128:(j + 1) * 128], kc_[:, j, :], self.ident(128), sig=(j == 1))
        k.tt(v3(ct["kg"], 64, 128), pkt[0:64, 0:256].re("p (j i) -> p j i", j=2), bc3(self.t_egc, 128), ALU.mult)
        k.tt(v3(ct["kdec"], 64, 128), pkt[0:64, 0:256].re("p (j i) -> p j i", j=2), bc3(self.t_edec, 128), ALU.mult)
        pvt = k.ps()
        for j in range(2):
            k.mm(pvt[0:64, j * 128:(j + 1) * 128], vc[:, j, :], self.ident(128), sig=(j == 1))
        k.copy(ct["vtm"][0:64, 0:256], pvt[0:64, 0:256], e="act")
        pw = k.ps()
        for j in range(2):
            k.mm(pw[:, j * 64:(j + 1) * 64], ct["kg"][0:64, j * 128:(j + 1) * 128], js(YT, j), sig=(j == 1))
        k.act(ct["wnt"][:, 0:128], pw[:, 0:128], AF.Copy, scale=-1.0)
        k.tt(v3(ct["qgt"], 128), qc, v3(ct["egrow"], 128), ALU.mult)
        for j in range(2):
            h = gi * 2 + j
            if self.smp:
                Sb = self.S_s
                k.dma("act", Sb, self.I["st_delta"][:, h].rearrange("s k v -> k s v"))
                Sv = lambda s: Sb[:, s, :]
                eg = lambda s: self.t_egl[:, s * 8 + h:s * 8 + h + 1]
            else:
                Sv = lambda s: self.S_p[:, h, :]
                eg = lambda s: self.t_egl[:, c * 8 + h:c * 8 + h + 1]
            wn_j = ct["wnt"][:, j * 64:(j + 1) * 64]
            qg_j = ct["qgt"][:, j * 64:(j + 1) * 64]
            if self.smp:
                wm = self.msk[:, 0:1024].re("p (s i) -> p s i", s=16)
                k.tt(wm, wn_j[:, None, :].bc([128, 16, 64]), self.SM3, ALU.mult)
            pv = k.ps()
            k.mm(pv[0:64, 0:128], js(YT, j), ct["vtm"][0:64, j * 128:(j + 1) * 128], True, False, sig=False)
            for s in range(nseq):
                lhs = wm[:, s, :] if self.smp else wn_j
                k.mm(pv[0:64, 0:128], lhs, Sv(s), False, s == nseq - 1)
            vn_j = ct["vnew"][0:64, j * 128:(j + 1) * 128]
            k.ts(vn_j, pv[0:64, 0:128], self.t_b[0:64, n0 + j:n0 + j + 1], None, ALU.mult)
            po = k.ps()
            k.mm(po[:, 0:64], vn_j, js(ct["qkm"], j), True, False, sig=False)
            for s in range(nseq):
                cols = slice(s * 4, s * 4 + 4) if self.smp else slice(0, 64)
                k.mm(po[:, cols], Sv(s), qg_j[:, cols], False, s == nseq - 1, sig=(s == nseq - 1))
            k.copy(oT[:, j, cs], po[:, 0:64], e="act")
            kd_j = ct["kdec"][0:64, j * 128:(j + 1) * 128]
            for half in range(2 if self.smp else 1):
                if self.smp:
                    km = self.msk[0:64, 0:1024].re("p (s i) -> p s i", s=8)
                    k.tt(km, kd_j[:, None, :].bc([64, 8, 128]),
                         self.RM[:, half * 8:(half + 1) * 8][:, :, None].bc([64, 8, 128]), ALU.mult)
                for s8 in range(8 if self.smp else 1):
                    s = half * 8 + s8
                    lhs = km[:, s8, :] if self.smp else kd_j
                    pss = k.ps()
                    k.mm(pss[:, 0:128], lhs, vn_j)
                    k.stt(Sv(s), Sv(s), eg(s), pss[:, 0:128], ALU.mult, ALU.add)
            if self.smp:
                k.dma("act", self.O["delta_s"][:, h].rearrange("s k v -> k s v"), Sb)

    def mixer_o(self):
        k = self.k
        I = self.I
        NT, nch = self.NT, self.nch
        wv = I["w_in_o"].rearrange("(kc p) n -> p kc n", p=128)
        ab3 = self.tok_scalars(wv)
        nb = nch * 8
        ig3 = self.t_g[0:64, 0:nb].re("p (c h) -> p c h", h=8)
        lf3 = self.t_b[0:64, 0:nb].re("p (c h) -> p c h", h=8)
        k.tt(ig3, ab3[:, :, 0:8], self.hp[0:64, None, 16:24].bc([64, nch, 8]), ALU.add)
        k.tt(lf3, ab3[:, :, 8:16], self.hp[0:64, None, 24:32].bc([64, nch, 8]), ALU.add)
        k.act(lf3, lf3, AF.Exp, scale=-1.0)
        k.act(lf3, lf3, AF.Ln, bias=1.0)
        lf2 = self.t_b[0:64, 0:nb]
        k.ts(lf2, lf2, -1.0)
        pst = k.ps()
        k.mm(pst[0:64, 0:nb], self.MUd, lf2)
        k.copy(self.t_gc[0:64, 0:nb], pst[0:64, 0:nb])
        pst = k.ps()
        k.mm(pst[0:64, 0:nb], self.SLm, lf2)
        k.tt(self.t_gend[0:64, 0:nb], pst[0:64, 0:nb], self.t_gc[0:64, 0:nb], ALU.subtract)
        k.tt(self.t_gend[0:64, 0:nb], self.t_gend[0:64, 0:nb], self.t_g[0:64, 0:nb], ALU.add)
        self.seq_rep(lf2, self.t_egl, None)
        for pi in range(4):
            qT, kT = [self.G[i][:, 0:2 * NT].re("p (j n) -> p j n", j=2) for i in range(2)]
            ktm = V(self.GG[0:64, 2048:2048 + nch * 256], [("G", 2), ("G", 3)]).re("p (c n) -> p c n", n=256)
            wt = self.wload(wv[:, :, pi * 256:pi * 256 + 256], KC, 256)
            for cc in range(2):
                pst = k.ps()
                for kc in range(KC):
                    k.mm(pst[:, 0:NT], wt[:, kc, cc * 128:(cc + 1) * 128], self.xb[:, kc, 0:NT], kc == 0, kc == KC - 1, sig=(kc == KC - 1))
                k.copy(qT[:, cc, :], pst[:, 0:NT], e="act")
            wt = self.wload(wv[:, :, 1024 + pi * 256:1024 + pi * 256 + 256], KC, 256)
            for cc in range(2):
                pst = k.ps()
                for kc in range(KC):
                    k.mm(pst[:, 0:NT], wt[:, kc, cc * 128:(cc + 1) * 128], self.xb[:, kc, 0:NT], kc == 0, kc == KC - 1, sig=(kc == KC - 1))
                k.act(kT[:, cc, :], pst[:, 0:NT], AF.Copy, scale=128.0 ** -0.5)
            for c in range(nch):
                pst = k.ps()
                for kc in range(KC):
                    k.mm(pst[0:64, 0:256], self.xb[:, kc, c * 64:(c + 1) * 64], wt[:, kc, :], kc == 0, kc == KC - 1, sig=(kc == KC - 1))
                k.ts(ktm[:, c, :], pst[0:64, 0:256], 128.0 ** -0.5)
            for j in range(2):
                h = pi * 2 + j
                wvt = self.wload(wv[:, :, 2048 + h * 256:2048 + h * 256 + 256], KC, 256)
                wot = self.wload(wv[:, :, 4096 + h * 256:4096 + h * 256 + 256], KC, 256)
                if self.smp:
                    self.load_cn_sample(h)
                for c in range(nch):
                    self.mlstm_chunk(h, j, c, qT, kT, ktm, wvt, wot)
                if self.smp:
                    self.store_cn_sample(h)

    def mlstm_chunk(self, h, j, c, qT, kT, ktm, wvt, wot):
        k = self.k
        nseq = self.nseq
        ct = self.ct
        cs = slice(c * 64, (c + 1) * 64)
        n = c * 8 + h
        col = lambda t: t[0:64, n:n + 1]
        vps = k.ps()
        for kc in range(KC):
            k.mm(vps[0:64, 0:256], self.xb[:, kc, cs], wvt[:, kc, :], kc == 0, kc == KC - 1, sig=(kc == KC - 1))
        vaug = self.vaug_t[0:64, 0:257]
        k.copy(vaug[:, 0:256], vps[0:64, 0:256], e="act")
        ops = k.ps()
        for kc in range(KC):
            k.mm(ops[0:64, 0:256], self.xb[:, kc, cs], wot[:, kc, :], kc == 0, kc == KC - 1, sig=(kc == KC - 1))
        osig = ct["osig"][0:64, 0:256]
        k.act(osig, ops[0:64, 0:256], AF.Sigmoid)
        dl = self.sm()
        k.ts(dl[0:64, 0:64], self.MUd, col(self.t_b), None, ALU.mult)
        k.stt(dl[0:64, 0:64], self.I64, col(self.t_g), dl[0:64, 0:64], ALU.mult, ALU.subtract)
        k.ts(dl[0:64, 64:128], self.SLm, col(self.t_b), None, ALU.mult)
        pr = k.ps()
        k.mm(pr[:, 0:128], self.ones[0:64, :], dl[0:64, 0:128])
        rn = ct["rn"]
        k.copy(rn[:, 0:128], pr[:, 0:128])
        dmat = self.sm()
        k.ts(dmat[0:64, 0:64], rn[0:64, 0:64], col(self.t_gc), None, ALU.add)
        k.tt(dmat[0:64, 0:64], dmat[0:64, 0:64], self.NEGmT, ALU.add)
        sv = self.sm()
        rmx, mtok, inter, mt, e_, emt, dn, ssum, wcol, mnt = [sv[0:64, i:i + 1] for i in range(10)]
        k.reduce(rmx, dmat[0:64, 0:64], ALU.max)
        if self.smp:
            m3 = self.mst[:, 0:128].re("p (s h) -> p s h", h=8)[:, :, h]
            gl3 = self.t_egl[:, 0:128].re("p (s h) -> p s h", h=8)[:, :, h]
            tmp = self.sm()
            k.tt(tmp[0:64, 0:16], m3[0:64, :], self.RM, ALU.mult)
            k.reduce(mtok, tmp[0:64, 0:16], ALU.add)
        else:
            m3 = self.mst[:, h:h + 1]
            gl3 = self.t_egl[:, n:n + 1]
            k.copy(mtok, m3[0:64, :])
        k.tt(inter, col(self.t_gc), mtok, ALU.add)
        k.tt(mt, inter, rmx, ALU.max)
        k.tt(e_, inter, mt, ALU.subtract)
        k.act(e_, e_, AF.Exp)
        k.act(emt, mt, AF.Exp, scale=-1.0)
        k.ts(dmat[0:64, 0:64], dmat[0:64, 0:64], mt, None, ALU.subtract)
        k.act(dmat[0:64, 0:64], dmat[0:64, 0:64], AF.Exp)
        pqk = k.ps()
        k.mm(pqk[0:64, 0:64], qT[:, j, cs], kT[:, j, cs])
        smat = self.sm()
        k.tt(smat[0:64, 0:64], pqk[0:64, 0:64], dmat[0:64, 0:64], ALU.mult)
        pst_ = k.ps()
        k.mm(pst_[0:64, 0:64], smat[0:64, 0:64], self.I64)
        sT = self.sm()
        k.copy(sT[0:64, 0:64], pst_[0:64, 0:64], e="act")
        if self.smp:
            Cv = lambda s: self.Cn_s[:, s, :]
            qm = self.msk[:, 0:1024].re("p (s i) -> p s i", s=16)
            k.tt(qm, qT[:, j, cs][:, None, :].bc([128, 16, 64]), self.SM3, ALU.mult)
        else:
            Cv = lambda s: self.Cn_p[:, h, :]
        p1 = k.ps()
        for s in range(nseq):
            lhs = qm[:, s, :] if self.smp else qT[:, j, cs]
            k.mm(p1[0:64, 0:257], lhs, Cv(s), s == 0, s == nseq - 1)
        p2 = k.ps()
        k.mm(p2[0:64, 0:257], sT[0:64, 0:64], vaug)
        nd = self.nd_t[0:64, 0:257]
        k.ts(nd, p1[0:64, 0:257], e_, None, ALU.mult)
        k.tt(nd, nd, p2[0:64, 0:257], ALU.add)
        k.act(dn, nd[:, 256:257], AF.Abs)
        k.tt(dn, dn, emt, ALU.max)
        k.recip(dn, dn)
        hr = ct["hr"][0:64, 0:256]
        k.ts(hr, nd[:, 0:256], dn, None, ALU.mult)
        junk = ct["yt"][0:64, 0:256]
        k.tt(junk, hr, hr, ALU.mult)
        k.reduce(ssum, junk, ALU.add)
        k.act(ssum, ssum, AF.Sqrt, bias=RMS_EPS, scale=1.0 / 256)
        k.recip(ssum, ssum)
        yt = ct["yt"][0:64, 0:256]
        k.stt(yt, hr, ssum, osig, ALU.mult, ALU.mult)
        for hf in range(2):
            pt = k.ps()
            k.mm(pt[:, 0:64], yt[:, hf * 128:(hf + 1) * 128], self.I64)
            k.ts(self.mixb[:, h * 2 + hf, cs], pt[:, 0:64], self.pp3[:, 57 + h * 2 + hf:58 + h * 2 + hf], None, ALU.mult)
        gr = self.sm()
        k.tt(gr[:, 0:64], rn[:, 0:64], rn[:, 64:128], ALU.add)
        mx = self.sm()
        T_ = 64 // nseq
        k.reduce(mx[:, 0:nseq], gr[:, 0:64].re("p (s t) -> p s t", t=T_), ALU.max)
        t1 = self.sm()
        k.tt(t1[:, 0:nseq], gl3, m3, ALU.add)
        mnew = self.sm()
        k.tt(mnew[:, 0:nseq], t1[:, 0:nseq], mx[:, 0:nseq], ALU.max)
        scf = self.sm()
        k.tt(scf[:, 0:nseq], t1[:, 0:nseq], mnew[:, 0:nseq], ALU.subtract)
        k.act(scf[:, 0:nseq], scf[:, 0:nseq], AF.Exp)
        if self.smp:
            tmp = self.sm()
            k.tt(tmp[0:64, 0:16], mnew[0:64, 0:16], self.RM, ALU.mult)
            k.reduce(mnt, tmp[0:64, 0:16], ALU.add)
        else:
            k.copy(mnt, mnew[0:64, 0:1])
        k.tt(wcol, col(self.t_gend), mnt, ALU.subtract)
        k.act(wcol, wcol, AF.Exp)
        vw = self.vw_t[0:64, 0:257]
        k.ts(vw, vaug, wcol, None, ALU.mult)
        kt_j = ktm[:, c, j * 128:(j + 1) * 128]
        for half in range(2 if self.smp else 1):
            if self.smp:
                km = self.msk[0:64, 0:1024].re("p (s i) -> p s i", s=8)
                k.tt(km, kt_j[:, None, :].bc([64, 8, 128]),
                     self.RM[:, half * 8:(half + 1) * 8][:, :, None].bc([64, 8, 128]), ALU.mult)
            for s8 in range(8 if self.smp else 1):
                s = half * 8 + s8
                lhs = km[:, s8, :] if self.smp else kt_j
                p3 = k.ps()
                k.mm(p3[:, 0:257], lhs, vw)
                k.stt(Cv(s), Cv(s), scf[:, s:s + 1], p3[:, 0:257], ALU.mult, ALU.add)
        k.copy(m3, mnew[:, 0:nseq])

    def load_sample_states(self):
        k = self.k
        I = self.I
        allx = [("xres", i) for i in range(KC)]
        self.S_s = V(self.xres.t[:, :, 64:192], [("S_s", 0)])
        self.Cn_s = V(self.xres.t[:, :, 192:449], [("Cn_s", 0)])
        for nk in (("S_s", 0), ("Cn_s", 0)):
            rd = {}
            for ok in allx:
                w = k.lastw.get(ok)
                for tok in ([w] if w else []) + list(k.readers.get(ok, {}).values()):
                    o = rd.get(tok[0].name)
                    if o is None or o[1] < tok[1]:
                        rd[tok[0].name] = tok
            k.readers[nk] = rd
        scv = I["st_conv"]
        for pc in range(4):
            b = self.io()
            k.dma("sp", b[0:48, :], scv[:, pc * 1024:(pc + 1) * 1024])
            for c4 in range(2):
                pst = k.ps()
                for j in range(4):
                    self.trans(pst[:, j * 48:(j + 1) * 48], b[0:48, (c4 * 4 + j) * 128:(c4 * 4 + j + 1) * 128])
                for j in range(4):
                    k.copy(self.carry_s[:, pc * 8 + c4 * 4 + j, :], pst[:, j * 48:(j + 1) * 48])
        b = self.io()
        k.dma("sp", b[0:16, :], I["st_lru"])
        pst = k.ps()
        for blk in range(8):
            self.trans(pst[:, blk * 16:(blk + 1) * 16], b[0:16, blk * 128:(blk + 1) * 128])
        k.copy(self.hl[:, :, :], pst[:, 0:128].re("p (b s) -> p b s", s=16))
        k.dma("sp", self.mst, I["st_mm"].rearrange("s h -> (s h)").rearrange("(o n) -> o n", o=1).to_broadcast([128, 128]))

    def load_cn_sample(self, h):
        k = self.k
        I = self.I
        for s in range(16):
            stg = self.sc()
            k.dma("act", stg[:, 0:256].re("p (a k) -> p a k", a=2), I["st_mc"][s, h].rearrange("(a p) k -> p a k", p=128))
            pst = k.ps()
            for a in range(2):
                self.trans(pst[:, a * 128:(a + 1) * 128], stg[:, a * 128:(a + 1) * 128])
            k.copy(self.Cn_s[:, s, 0:256], pst[:, 0:256])
        b = self.sc()
        k.dma("act", b[0:16, 0:128], I["st_mn"][:, h, :])
        pst = k.ps()
        self.trans(pst[:, 0:16], b[0:16, 0:128])
        k.copy(self.Cn_s[:, :, 256], pst[:, 0:16])

    def store_cn_sample(self, h):
        k = self.k
        O = self.O
        for s in range(16):
            pst = k.ps()
            for a in range(2):
                self.trans(pst[:, a * 128:(a + 1) * 128], self.Cn_s[:, s, a * 128:(a + 1) * 128])
            stg = self.sc()
            k.copy(stg[:, 0:256], pst[:, 0:256], e="act")
            k.dma("act", O["mc_s"][s, h].rearrange("(a p) k -> p a k", p=128), stg[:, 0:256].re("p (a k) -> p a k", a=2))
        pst = k.ps()
        self.trans(pst[0:16, 0:128], self.Cn_s[:, :, 256])
        b = self.sc()
        k.copy(b[0:16, 0:128], pst[0:16, 0:128])
        k.dma("act", O["mn_s"][:, h, :], b[0:16, 0:128])

    def store_prompt_states(self):
        k = self.k
        O = self.O
        k.dma("sp", O["delta_p"].rearrange("h k v -> k h v"), self.S_p[:, :, :])
        stg = self.io()
        pst = k.ps()
        for r in range(3):
            self.trans(pst[0:32, r * 128:(r + 1) * 128], self.carry_p[:, :, r])
        k.copy(stg[0:32, 0:384], pst[0:32, 0:384])
        k.dma("sp", O["conv_p"].rearrange("r (c p) -> c r p", p=128), stg[0:32, 0:384].re("c (r p) -> c r p", r=3))
        pst = k.ps()
        self.trans(pst[0:8, 0:128], self.hl[:, :, 0])
        k.copy(stg[0:8, 384:512], pst[0:8, 0:128])
        k.dma("sp", O["lru_p"], stg[0:8, 384:512])
        for h in range(8):
            pst = k.ps()
            for a in range(2):
                self.trans(pst[:, a * 128:(a + 1) * 128], self.Cn_p[:, h, a * 128:(a + 1) * 128])
            s2 = self.sc()
            k.copy(s2[:, 0:256], pst[:, 0:256], e="act")
            k.dma("sp", O["mc_p"][h].rearrange("(a p) k -> p a k", p=128), s2[:, 0:256].re("p (a k) -> p a k", a=2))
        pst = k.ps()
        self.trans(pst[0:8, 0:128], self.Cn_p[:, :, 256])
        k.copy(stg[0:8, 512:640], pst[0:8, 0:128])
        k.dma("sp", O["mn_p"], stg[0:8, 512:640])
        k.dma("sp", O["mm_p"], self.mst[0:1, 0:8])

    def store_sample_states(self):
        k = self.k
        O = self.O
        cso = O["conv_s"].rearrange("s r n -> (s r) n")
        for pc in range(4):
            b = self.io()
            for c4 in range(2):
                pst = k.ps()
                for j in range(4):
                    self.trans(pst[0:48, j * 128:(j + 1) * 128], self.carry_s[:, pc * 8 + c4 * 4 + j, :])
                k.copy(b[0:48, c4 * 512:(c4 + 1) * 512], pst[0:48, :])
            k.dma("sp", cso[:, pc * 1024:(pc + 1) * 1024], b[0:48, :])
        b = self.io()
        pst = k.ps()
        pst2 = k.ps()
        for blk in range(8):
            p_ = pst if blk < 4 else pst2
            self.trans(p_[0:16, (blk % 4) * 128:(blk % 4 + 1) * 128], self.hl[:, blk, :])
        k.copy(b[0:16, 0:512], pst[0:16, :])
        k.copy(b[0:16, 512:1024], pst2[0:16, :])
        k.dma("sp", O["lru_s"], b[0:16, :])
        k.dma("sp", O["mm_s"].rearrange("s h -> (s h)").rearrange("(o n) -> o n", o=1), self.mst[0:1, 0:128])


IN_SPECS = [("x_p", (2048, D)), ("x_s", (64, D)), ("p_p", (2, 2048, 256)), ("p_s", (2, 64, 256)),
            ("st_conv", (48, 4096)), ("st_delta", (16, 8, 128, 128)), ("st_lru", (16, 1024)),
            ("st_mc", (16, 8, 256, 128)), ("st_mn", (16, 8, 128)), ("st_mm", (16, 8)),
            ("w_in_e", (D, 6160)), ("w_conv_e", (4, 4096)), ("b_conv_e", (1, 4096)), ("a_log_e", (1, 8)),
            ("dt_bias_e", (1, 8)), ("delta_norm_e", (1, 128)), ("lru_wr_e", (8, 128, 128)),
            ("lru_br_e", (1, 1024)), ("lru_wi_e", (8, 128, 128)), ("lru_bi_e", (1, 1024)),
            ("lru_lambda_e", (1, 1024)), ("w_out_e", (D, D)), ("w_in_o", (D, 6160)), ("b_ig_o", (1, 8)),
            ("b_fg_o", (1, 8)), ("mlstm_norm_o", (1, 2048)), ("w_out_o", (D, D)),
            ("ln1_g", (2, D)), ("ln1_b", (2, D)), ("ln2_g", (2, D)), ("ln2_b", (2, D)),
            ("w_up", (2, D, 8192)), ("w_down", (2, 8192, D)), ("w_ple", (2, 256, D)),
            ("w_ple_gate", (2, D, D)), ("consts", (128, C_END))]
OUT_SPECS = [("y_p", (2048, D)), ("y_s", (64, D)), ("conv_p", (3, 4096)), ("delta_p", (8, 128, 128)),
             ("lru_p", (8, 128)), ("mc_p", (8, 256, 128)), ("mn_p", (8, 128)), ("mm_p", (1, 8)),
             ("conv_s", (16, 3, 4096)), ("delta_s", (16, 8, 128, 128)), ("lru_s", (16, 1024)),
             ("mc_s", (16, 8, 256, 128)), ("mn_s", (16, 8, 128)), ("mm_s", (16, 8))]


def make_consts():
    c = np.zeros((128, C_END), np.float32)
    c[:, C_ID:C_ID + 128] = np.eye(128, dtype=np.float32)
    m = np.arange(64)
    for base, seqlen in ((C_P, 64), (C_S, 4)):
        same = (m[:, None] // seqlen) == (m[None, :] // seqlen)
        mud = same & (m[:, None] <= m[None, :])
        mu = same & (m[:, None] < m[None, :])
        c[:64, base:base + 64] = mud
        c[:64, base + 64:base + 128] = mu
        c[:64, base + 128:base + 192] = same
        c[:64, base + 192:base + 256] = np.where(mud, 0.0, NEG)
        nt0 = C_NTP if base == C_P else C_NTS
        c[:64, nt0:nt0 + 64] = np.where(mud.T, 0.0, NEG)
    c[:64, C_RM:C_RM + 16] = (m[:, None] // 4) == np.arange(16)[None, :]
    sm = (np.arange(16)[:, None] == (m[None, :] // 4)).astype(np.float32).reshape(1, 1024)
    c[:, C_SM:C_SM + 1024] = sm
    return c


_CACHE = {}


def kernel(**inp):
    f = lambda a: np.ascontiguousarray(np.asarray(a, dtype=np.float32))
    if "prog" not in _CACHE:
        _CACHE["prog"] = Prog()
    prog = _CACHE["prog"]
    consts = make_consts()
    shared = {}
    for n in ("w_in_e", "w_conv_e", "b_conv_e", "a_log_e", "dt_bias_e", "delta_norm_e", "lru_wr_e", "lru_br_e",
              "lru_wi_e", "lru_bi_e", "lru_lambda_e", "w_out_e", "w_in_o", "b_ig_o", "b_fg_o", "mlstm_norm_o", "w_out_o"):
        a = f(inp[n])[0]
        if a.ndim == 1:
            a = a[None]
        shared[n] = np.ascontiguousarray(a)
    for n in ("ln1_g", "ln1_b", "ln2_g", "ln2_b", "w_up", "w_down", "w_ple", "w_ple_gate"):
        shared[n] = f(inp[n])
    shared["consts"] = consts
    in_maps = []
    for c in range(8):
        b = c % 4
        sl = slice(16 * c, 16 * c + 16)
        m = dict(shared)
        m["x_p"] = f(inp["x_prompt"][b])
        m["x_s"] = f(inp["x_sample"][sl]).reshape(64, D)
        m["p_p"] = f(inp["p_prompt"][:, b])
        m["p_s"] = f(inp["p_sample"][:, sl]).reshape(2, 64, 256)
        m["st_conv"] = f(inp["state_conv"][0, sl]).reshape(48, 4096)
        m["st_delta"] = f(inp["state_delta"][0, sl])
        m["st_lru"] = f(inp["state_lru"][0, sl])
        m["st_mc"] = f(inp["state_mlstm_c"][0, sl])
        m["st_mn"] = f(inp["state_mlstm_n"][0, sl])
        m["st_mm"] = f(inp["state_mlstm_m"][0, sl])
        in_maps.append(m)
    import os
    ncores = int(os.environ.get("KCORES", "8"))
    res = run_bass_kernel_spmd(prog.nc, in_maps[:ncores], core_ids=list(range(ncores)))
    R = list(res.results)
    while len(R) < 8:
        R.append(R[0])
    cat = lambda n: np.concatenate([R[c][n] for c in range(8)], axis=0)
    stk = lambda n: np.stack([R[c][n] for c in range(4)], axis=0)
    y_p = stk("y_p")
    y_s = cat("y_s").reshape(128, 4, D)
    outs = (y_p, y_s,
            stk("conv_p")[None], stk("delta_p")[None], stk("lru_p").reshape(1, 4, 1024), stk("mc_p")[None],
            stk("mn_p")[None], stk("mm_p").reshape(1, 4, 8),
            cat("conv_s")[None], cat("delta_s")[None], cat("lru_s")[None], cat("mc_s")[None],
            cat("mn_s")[None], cat("mm_s")[None])
    return tuple(np.ascontiguousarray(o, dtype=np.float32) for o in outs)
```

```python
import numpy as np
from contextlib import ExitStack
import concourse.bass as bass
import concourse.mybir as mybir
from concourse.bass_utils import run_bass_kernel_spmd

F32 = mybir.dt.float32
BF16 = mybir.dt.bfloat16
AF = mybir.ActivationFunctionType
ALU = mybir.AluOpType
AX = mybir.AxisListType

D = 2048
KC = 16
L = 64
NTP = 512
WT = 256
NTS = 64
NPASS_P = 4
ALPHA = 4.0 ** 0.25
LN_EPS = 1e-5
RMS_EPS = 1e-6
NEG = -1.0e30

C_ID = 0
C_P = 128
C_S = 384
C_RM = 640
C_SM = 656
C_NTP = 656 + 1024
C_NTS = C_NTP + 64
C_END = C_NTS + 64


class V:
    __slots__ = ("ap", "keys")

    def __init__(self, ap, keys=()):
        self.ap = ap
        self.keys = tuple(keys)

    def __getitem__(self, idx):
        return V(self.ap[idx], self.keys)

    def re(self, pat, **kw):
        return V(self.ap.rearrange(pat, **kw), self.keys)

    def bc(self, shape):
        return V(self.ap.to_broadcast(shape), self.keys)


class TL:
    def __init__(self, t, name, nslots):
        self.t = t
        self.name = name
        self.n = nslots

    def __getitem__(self, idx):
        if not isinstance(idx, tuple):
            idx = (idx,)
        if len(idx) < 2:
            ks = range(self.n)
        else:
            s = idx[1]
            if isinstance(s, int):
                ks = [s]
            else:
                ks = range(*s.indices(self.n))
        return V(self.t[idx], [(self.name, i) for i in ks])


def asv(x):
    return x if isinstance(x, V) else V(x, ())


class K:
    def __init__(self, nc, es):
        self.nc = nc
        self.es = es
        self.eng = dict(pe=nc.tensor, act=nc.scalar, dve=nc.vector, pool=nc.gpsimd, sp=nc.sync)
        self.sem = {e: es.enter_context(nc.semaphore("s_" + e)) for e in self.eng}
        self.cnt = {e: 0 for e in self.eng}
        self.known = {e: {} for e in self.eng}
        self.lastw = {}
        self.readers = {}
        self.dsem = {}
        for q, n in (("sp", 20), ("act", 6), ("pool", 6)):
            self.dsem[q] = [[es.enter_context(nc.semaphore(f"d_{q}{i}")), 0] for i in range(n)]
        self.dnext = {q: 0 for q in self.dsem}
        self.psb = []
        for i in range(8):
            t = es.enter_context(nc.psum_tensor(f"psb{i}", [128, 512], F32))
            self.psb.append(V(t[:], [("psb", i)]))
        self.psn = 0
        self.ntile = 0
        self.log = {e: [] for e in self.eng}

    def tile(self, name, shape, dt=F32, slots=0):
        t = self.es.enter_context(self.nc.sbuf_tensor(name, list(shape), dt))
        if slots:
            return TL(t, name, slots)
        return V(t[:], [(name, 0)])

    def ps(self):
        v = self.psb[self.psn % 8]
        self.psn += 1
        return v

    def _deps(self, outs, ins):
        deps = {}

        def add(tok):
            sem, val = tok
            o = deps.get(sem.name)
            if o is None or o[1] < val:
                deps[sem.name] = (sem, val)

        for v in ins:
            for kk in v.keys:
                w = self.lastw.get(kk)
                if w:
                    add(w)
                if kk[0] == "psb":
                    for r in self.readers.get(kk, {}).values():
                        add(r)
        for v in outs:
            for kk in v.keys:
                w = self.lastw.get(kk)
                if w:
                    add(w)
                for r in self.readers.get(kk, {}).values():
                    add(r)
        return deps

    def _wait(self, e, deps):
        kn = self.known[e]
        for name, (sem, val) in deps.items():
            if e == "pe" and name == "s_pe":
                continue
            if kn.get(name, 0) >= val:
                continue
            self.eng[e].wait_ge(sem, val)
            self.log[e].append(("w", name, val))
            kn[name] = val

    def _record(self, tok, outs, ins):
        sem = tok[0]
        for v in outs:
            for kk in v.keys:
                self.lastw[kk] = tok
                self.readers[kk] = {}
        for v in ins:
            for kk in v.keys:
                self.readers.setdefault(kk, {})[sem.name] = tok

    def op(self, e, fn, outs, ins, sig=True):
        outs = [asv(o) for o in outs]
        ins = [asv(i) for i in ins if isinstance(i, V)]
        self._wait(e, self._deps(outs, ins))
        inst = fn(self.eng[e])
        if sig:
            inst.then_inc(self.sem[e], 1)
            self.log[e].append(("i", "s_" + e, 1))
            self.cnt[e] += 1
            tok = (self.sem[e], self.cnt[e])
        else:
            tok = (self.sem[e], self.cnt[e] + 1)
        self._record(tok, outs, ins)

    def dma(self, q, out, in_):
        out = asv(out)
        in_ = asv(in_)
        deps = self._deps([out], [in_])
        pool = self.dsem[q]
        slot = pool[self.dnext[q] % len(pool)]
        self.dnext[q] += 1
        sem, tot = slot
        if tot:
            deps[sem.name] = (sem, tot)
        self._wait(q, deps)
        self.eng[q].dma_start(out=out.ap, in_=in_.ap).then_inc(sem, 16)
        self.log[q].append(("i", sem.name, 16))
        slot[1] = tot + 16
        self._record((sem, tot + 16), [out], [in_])

    def finish(self):
        deps = {}
        for q in self.dsem:
            for sem, tot in self.dsem[q]:
                if tot:
                    deps[sem.name] = (sem, tot)
        for e in ("pe", "act", "dve"):
            if self.cnt[e]:
                deps["s_" + e] = (self.sem[e], self.cnt[e])
        self._wait("sp", dict(deps))

    def mm(self, out, lhsT, rhs, start=True, stop=True, sig=None):
        if sig is None:
            sig = True
        self.op("pe", lambda e: e.matmul(out.ap, lhsT=lhsT.ap, rhs=rhs.ap, start=start, stop=stop),
                [out], [lhsT, rhs], sig=sig)

    def act(self, out, in_, func, bias=None, scale=1.0, accum=None):
        ins = [in_]
        kw = {}
        if bias is not None:
            kw["bias"] = bias.ap if isinstance(bias, V) else bias
            if isinstance(bias, V):
                ins.append(bias)
        if isinstance(scale, V):
            ins.append(scale)
            kw["scale"] = scale.ap
        else:
            kw["scale"] = scale
        outs = [out]
        if accum is not None:
            kw["accum_out"] = accum.ap
            outs.append(accum)
        self.op("act", lambda e: e.activation(out=out.ap, in_=in_.ap, func=func, **kw), outs, ins)

    def tt(self, out, in0, in1, op, e="dve"):
        self.op(e, lambda g: g.tensor_tensor(out=out.ap, in0=in0.ap, in1=in1.ap, op=op), [out], [in0, in1])

    def ts(self, out, in0, s1, s2=None, op0=ALU.mult, op1=None, e="dve"):
        ins = [in0]
        a1 = s1.ap if isinstance(s1, V) else s1
        a2 = s2.ap if isinstance(s2, V) else s2
        if isinstance(s1, V):
            ins.append(s1)
        if isinstance(s2, V):
            ins.append(s2)
        if op1 is None:
            self.op(e, lambda g: g.tensor_scalar(out=out.ap, in0=in0.ap, scalar1=a1, scalar2=None, op0=op0), [out], ins)
        else:
            self.op(e, lambda g: g.tensor_scalar(out=out.ap, in0=in0.ap, scalar1=a1, scalar2=a2, op0=op0, op1=op1), [out], ins)

    def stt(self, out, in0, scalar, in1, op0, op1):
        ins = [in0, in1]
        sc = scalar.ap if isinstance(scalar, V) else scalar
        if isinstance(scalar, V):
            ins.append(scalar)
        self.op("dve", lambda g: g.scalar_tensor_tensor(out=out.ap, in0=in0.ap, scalar=sc, in1=in1.ap, op0=op0, op1=op1),
                [out], ins)

    def copy(self, out, in_, e="dve"):
        if e == "act":
            self.op("act", lambda g: g.activation(out=out.ap, in_=in_.ap, func=AF.Copy), [out], [in_])
        else:
            self.op(e, lambda g: g.tensor_copy(out=out.ap, in_=in_.ap), [out], [in_])

    def memset(self, out, val, e="dve"):
        self.op(e, lambda g: g.memset(out.ap, val), [out], [])

    def recip(self, out, in_):
        self.op("dve", lambda g: g.reciprocal(out=out.ap, in_=in_.ap), [out], [in_])

    def reduce(self, out, in_, op):
        self.op("dve", lambda g: g.tensor_reduce(out=out.ap, in_=in_.ap, axis=AX.X, op=op), [out], [in_])

    def scan(self, out, d0, d1, init):
        self.op("dve", lambda g: g.tensor_tensor_scan(out=out.ap, data0=d0.ap, data1=d1.ap, initial=init.ap,
                                                       op0=ALU.mult, op1=ALU.add), [out], [d0, d1, init])


class StopBuild(Exception):
    pass


class Prog:
    def stop(self, stage):
        import os
        if os.environ.get("KSTOP") == stage:
            raise StopBuild()

    def __init__(self):
        nc = bass.Bass("TRN2", target_bir_lowering=False)
        self.nc = nc
        di = lambda n, s: nc.dram_tensor(n, list(s), F32, kind="ExternalInput").ap()
        do = lambda n, s: nc.dram_tensor(n, list(s), F32, kind="ExternalOutput").ap()
        I = self.I = {}
        for n, s in IN_SPECS:
            I[n] = di(n, s)
        O = self.O = {}
        for n, s in OUT_SPECS:
            O[n] = do(n, s)
        with ExitStack() as es:
            self.k = K(nc, es)
            self.alloc()
            self.setup()
            try:
                self.stop("setup")
                for ps in range(NPASS_P + 1):
                    self.run_pass(ps)
            except StopBuild:
                self.store_prompt_states()
            self.k.finish()

    def alloc(self):
        k = self.k
        self.xres = k.tile("xres", [128, KC, NTP], F32, slots=KC)
        self.xb = k.tile("xb", [128, KC, NTP], BF16, slots=KC)
        self.mixb = k.tile("mixb", [128, KC, NTP], BF16, slots=KC)
        self.wbuf = [k.tile(f"wbuf{i}", [128, KC * WT], BF16) for i in range(3)]
        self.wn = 0
        self.wpb = [k.tile(f"wpb{i}", [128, 2 * WT], BF16) for i in range(2)]
        self.wpn = 0
        self.pT = k.tile("pT", [128, 2, NTP], BF16)
        self.xrb = k.tile("xrb", [128, 2, NTP], BF16)
        self.cst = k.tile("cst", [128, C_END], F32)
        self.ones = k.tile("ones", [128, 128], F32)
        gg = k.es.enter_context(k.nc.sbuf_tensor("GG", [128, 4 * 1024], F32))
        self.GG = gg
        self.G = [V(gg[:, i * 1024:(i + 1) * 1024], [("G", i)]) for i in range(4)]
        self.U = [k.tile(f"U{i}", [128, 520], F32) for i in range(2)]
        self.un = 0
        self.scr = [k.tile(f"scr{i}", [128, 512], F32) for i in range(6)]
        self.sn = 0
        self.msk = k.tile("msk", [128, 1024], F32)
        self.cw = k.tile("cw", [128, 128], F32)
        self.lnp = k.tile("lnp", [128, 128], F32)
        self.pp3 = k.tile("pp3", [128, 128], F32)
        self.nsl = k.tile("nsl", [128, 16], F32)
        self.hp = k.tile("hp", [128, 32], F32)
        self.wrb = k.tile("wrb", [128, 8 * 128], BF16)
        self.wib = k.tile("wib", [128, 8 * 128], BF16)
        self.carry_p = k.tile("carry_p", [128, 32, 3], F32, slots=32)
        self.carry_s = k.tile("carry_s", [128, 32, 48], F32, slots=32)
        self.S_p = k.tile("S_p", [128, 8, 128], F32, slots=8)
        self.Cn_p = k.tile("Cn_p", [128, 8, 257], F32, slots=8)
        self.hl = k.tile("hl", [128, 8, 16], F32, slots=8)
        self.mst = k.tile("mst", [128, 128], F32)
        self.iobuf = [k.tile(f"iobuf{i}", [128, 1024], F32) for i in range(2)]
        self.ion = 0
        self.t_ab = k.tile("t_ab", [128, 128], F32)
        self.t_g = k.tile("t_g", [128, 64], F32)
        self.t_b = k.tile("t_b", [128, 64], F32)
        self.t_gc = k.tile("t_gc", [128, 64], F32)
        self.t_egc = k.tile("t_egc", [128, 64], F32)
        self.t_edec = k.tile("t_edec", [128, 64], F32)
        self.t_gend = k.tile("t_gend", [128, 64], F32)
        self.t_egl = k.tile("t_egl", [128, 128], F32)
        self.ct = {n: k.tile("ct_" + n, [128, 256], F32) for n in
                   ("egrow", "dmd", "bm", "qkm", "n0", "n0t", "kg", "kdec", "vtm", "vnew", "wnt", "qgt",
                    "osig", "hr", "yt", "rn")}
        self.vaug_t = k.tile("vaug_t", [128, 260], F32)
        self.vaug_t2 = k.tile("vaug_t2", [128, 260], F32)
        self.osig2 = k.tile("osig2", [128, 256], F32)
        self.nd_t = k.tile("nd_t", [128, 260], F32)
        self.vw_t = k.tile("vw_t", [128, 260], F32)
        self.npool = [k.tile(f"np{i}", [128, 128], F32) for i in range(12)]
        self.nn = 0
        self.small = [k.tile(f"sm{i}", [128, 128], F32) for i in range(10)]
        self.smn = 0

    def sc(self):
        v = self.scr[self.sn % len(self.scr)]
        self.sn += 1
        return v

    def npl(self):
        v = self.npool[self.nn % len(self.npool)]
        self.nn += 1
        return v

    def sm(self):
        v = self.small[self.smn % len(self.small)]
        self.smn += 1
        return v

    def io(self):
        v = self.iobuf[self.ion % 2]
        self.ion += 1
        return v

    def ident(self, n=128):
        return self.cst[0:n, C_ID:C_ID + n]

    def trans(self, out_ps, in_):
        P = in_.ap.shape[0]
        self.k.mm(out_ps, in_, self.ident(P), True, True)

    def wload(self, view, kcn, ncols):
        buf = self.wbuf[self.wn % 3]
        self.wn += 1
        dst = V(buf.ap[:, 0:kcn * ncols].rearrange("p (k n) -> p k n", k=kcn), buf.keys)
        self.k.dma("pool", dst, view)
        return dst

    def setup(self):
        k = self.k
        I = self.I
        k.dma("sp", self.cst, I["consts"])
        k.memset(self.ones, 1.0)
        k.memset(self.vaug_t, 1.0)
        k.memset(self.vaug_t2, 1.0)
        st = self.io()
        r128 = lambda ap: ap.rearrange("j (c p) -> (j c) p", p=128)
        k.dma("sp", st[:, 0:128], r128(I["w_conv_e"]))
        k.dma("sp", st[0:32, 128:256], r128(I["ln1_g"]))
        k.dma("sp", st[32:64, 128:256], r128(I["ln1_b"]))
        k.dma("sp", st[64:96, 128:256], r128(I["ln2_g"]))
        k.dma("sp", st[96:128, 128:256], r128(I["ln2_b"]))
        k.memset(st[:, 256:384], 0.0)
        k.dma("sp", st[0:32, 256:384], r128(I["b_conv_e"]))
        k.dma("sp", st[32:40, 256:384], r128(I["lru_br_e"]))
        k.dma("sp", st[40:48, 256:384], r128(I["lru_bi_e"]))
        k.dma("sp", st[48:56, 256:384], r128(I["lru_lambda_e"]))
        k.dma("sp", st[56:57, 256:384], I["delta_norm_e"])
        k.dma("sp", st[57:73, 256:384], r128(I["mlstm_norm_o"]))
        for i, dst in enumerate((self.cw, self.lnp, self.pp3)):
            ps = k.ps()
            self.trans(ps[:, 0:128], st[:, i * 128:(i + 1) * 128])
            k.copy(dst, ps[:, 0:128])
        t = self.sm()
        k.act(t[:, 0:8], self.pp3[:, 48:56], AF.Exp, scale=-1.0)
        k.act(t[:, 0:8], t[:, 0:8], AF.Ln, bias=1.0)
        k.ts(self.nsl[:, 0:8], t[:, 0:8], -8.0)
        k.ts(self.nsl[:, 8:16], t[:, 0:8], -16.0)
        k.dma("sp", self.hp[0:64, 0:8], I["a_log_e"].to_broadcast([64, 8]))
        k.dma("sp", self.hp[0:64, 8:16], I["dt_bias_e"].to_broadcast([64, 8]))
        k.dma("sp", self.hp[0:64, 16:24], I["b_ig_o"].to_broadcast([64, 8]))
        k.dma("sp", self.hp[0:64, 24:32], I["b_fg_o"].to_broadcast([64, 8]))
        k.act(self.hp[0:64, 0:8], self.hp[0:64, 0:8], AF.Exp)
        k.ts(self.hp[0:64, 0:8], self.hp[0:64, 0:8], -1.0)
        k.dma("pool", self.wrb.re("p (n d) -> p n d", n=8), I["lru_wr_e"].rearrange("n c d -> c n d"))
        k.dma("pool", self.wib.re("p (n d) -> p n d", n=8), I["lru_wi_e"].rearrange("n c d -> c n d"))
        k.memset(self.carry_p[:, :, :], 0.0)
        k.memset(self.S_p[:, :, :], 0.0)
        k.memset(self.Cn_p[:, :, :], 0.0)
        k.memset(self.hl[:, :, :], 0.0)
        k.memset(self.mst, 0.0)

    def run_pass(self, ps):
        k = self.k
        I = self.I
        O = self.O
        smp = ps == NPASS_P
        NT = NTS if smp else NTP
        self.NT = NT
        self.smp = smp
        self.nseq = 16 if smp else 1
        self.T = 4 if smp else NTP
        self.nch = NT // L
        mc = C_S if smp else C_P
        c64 = lambda o: self.cst[0:64, mc + o:mc + o + 64]
        self.MUd, self.MU, self.SLm, self.NEGm = c64(0), c64(64), c64(128), c64(192)
        self.RM = self.cst[0:64, C_RM:C_RM + 16]
        nt0 = C_NTS if smp else C_NTP
        self.NEGmT = self.cst[0:64, nt0:nt0 + 64]
        self.SM3 = self.cst[:, C_SM:C_SM + 1024].re("p (s i) -> p s i", s=16)
        self.I64 = self.ident(64)
        xsrc = I["x_s"] if smp else I["x_p"][ps * NTP:(ps + 1) * NTP, :]
        ydst = O["y_s"] if smp else O["y_p"][ps * NTP:(ps + 1) * NTP, :]
        ntt = (NT + 127) // 128
        tsz = min(NT, 128)
        if smp:
            self.load_sample_states()
        import os
        kx = int(os.environ.get("KX", "99"))
        kxm = int(os.environ.get("KXM", "1"))
        kxc = int(os.environ.get("KXC", "1"))
        for tt in range(ntt):
            for hf in range(2):
                if tt * 2 + hf >= kx:
                    continue
                b = self.io()
                k.dma("sp", b[0:tsz, :], xsrc[tt * 128:tt * 128 + tsz, hf * 1024:(hf + 1) * 1024])
                for c4 in range(2):
                    if not kxm:
                        continue
                    pst = k.ps()
                    for j in range(4):
                        self.trans(pst[:, j * 128:j * 128 + tsz], b[0:tsz, (c4 * 4 + j) * 128:(c4 * 4 + j + 1) * 128])
                    for j in range(4):
                        if not kxc:
                            continue
                        c = hf * 8 + c4 * 4 + j
                        k.copy(self.xres[:, c, tt * 128:tt * 128 + tsz], pst[:, j * 128:j * 128 + tsz], e="act")
                        k.copy(self.xb[:, c, tt * 128:tt * 128 + tsz], self.xres[:, c, tt * 128:tt * 128 + tsz], e="act")
        if ps == 0:
            self.stop("x")
        for layer in range(2):
            if layer == 0:
                self.mixer_e()
                wo = I["w_out_e"]
                if ps == 0:
                    self.stop("e")
            else:
                self.mixer_o()
                wo = I["w_out_o"]
                if ps == 0:
                    self.stop("o")
            wv = wo.rearrange("(kc p) n -> p kc n", p=128)

            def ev_res(c, pst):
                k.stt(self.xres[:, c, 0:NT], self.xres[:, c, 0:NT], ALPHA, pst, ALU.mult, ALU.add)
            self.dense(wv, KC, D, self.mixb, ev_res)
            if ps == 0:
                self.stop("res%d" % layer)
            self.layer_norm(0 * 32 + layer * 16, 32 + layer * 16)
            if ps == 0:
                self.stop("ln1_%d" % layer)
            self.mlp(layer)
            if ps == 0:
                self.stop("mlp%d" % layer)
            self.layer_norm(64 + layer * 16, 96 + layer * 16)
            if ps == 0:
                self.stop("ln2_%d" % layer)
            self.ple(layer, ps)
            if ps == 0:
                self.stop("l%d" % layer)
        for tt in range(ntt):
            for hf in range(2):
                b = self.io()
                for c4 in range(2):
                    pst = k.ps()
                    for j in range(4):
                        c = hf * 8 + c4 * 4 + j
                        self.trans(pst[0:tsz, j * 128:(j + 1) * 128], self.xres[:, c, tt * 128:tt * 128 + tsz])
                    k.copy(b[0:tsz, c4 * 512:(c4 + 1) * 512], pst[0:tsz, :], e="act" if c4 % 2 else "dve")
                k.dma("sp", ydst[tt * 128:tt * 128 + tsz, hf * 1024:(hf + 1) * 1024], b[0:tsz, :])
        if ps == NPASS_P - 1:
            self.store_prompt_states()
        if smp:
            self.store_sample_states()

    def dense(self, wv, kcn, ncols, xin, evac):
        k = self.k
        NT = self.NT
        for t0 in range(0, ncols, WT):
            n_ = min(WT, ncols - t0)
            wt = self.wload(wv[:, :, t0:t0 + n_], kcn, n_)
            for cc in range(n_ // 128):
                pst = k.ps()
                for kc in range(kcn):
                    k.mm(pst[:, 0:NT], wt[:, kc, cc * 128:(cc + 1) * 128], xin[:, kc, 0:NT], kc == 0, kc == kcn - 1, sig=(kc == kcn - 1))
                evac(t0 // 128 + cc, pst[:, 0:NT])

    def layer_norm(self, goff, boff):
        k = self.k
        NT = self.NT
        prm = self.lnp
        p1 = k.ps()
        p2 = k.ps()
        for c in range(KC):
            sq = self.sc()
            k.act(sq[:, 0:NT], self.xres[:, c, 0:NT], AF.Square)
            k.mm(p1[:, 0:NT], self.ones, self.xres[:, c, 0:NT], c == 0, c == KC - 1)
            k.mm(p2[:, 0:NT], self.ones, sq[:, 0:NT], c == 0, c == KC - 1)
        mean, rstd, var = self.G[0], self.G[1], self.G[2]
        k.ts(mean[:, 0:NT], p1[:, 0:NT], 1.0 / D)
        k.tt(var[:, 0:NT], mean[:, 0:NT], mean[:, 0:NT], ALU.mult)
        k.stt(var[:, 0:NT], p2[:, 0:NT], 1.0 / D, var[:, 0:NT], ALU.mult, ALU.subtract)
        k.act(rstd[:, 0:NT], var[:, 0:NT], AF.Sqrt, bias=LN_EPS)
        k.recip(rstd[:, 0:NT], rstd[:, 0:NT])
        k.tt(mean[:, 0:NT], mean[:, 0:NT], rstd[:, 0:NT], ALU.mult)
        for c in range(KC):
            t = self.sc()
            k.tt(t[:, 0:NT], self.xres[:, c, 0:NT], rstd[:, 0:NT], ALU.mult)
            k.tt(t[:, 0:NT], t[:, 0:NT], mean[:, 0:NT], ALU.subtract)
            k.act(self.xres[:, c, 0:NT], t[:, 0:NT], AF.Identity, bias=prm[:, boff + c:boff + c + 1],
                  scale=prm[:, goff + c:goff + c + 1])
            k.copy(self.xb[:, c, 0:NT], self.xres[:, c, 0:NT], e="act")

    def mlp(self, layer):
        k = self.k
        NT = self.NT
        I = self.I
        wu = I["w_up"][layer].rearrange("(kc p) n -> p kc n", p=128)
        wd = I["w_down"][layer].rearrange("(kc p) n -> p kc n", p=128)
        hb = self.mixb
        for g in range(4):
            def ev_up(c, pst):
                sq = self.sc()
                k.act(sq[:, 0:NT], pst, AF.Square)
                k.stt(hb[:, c, 0:NT], pst, 0.0, sq[:, 0:NT], ALU.is_gt, ALU.mult)
            self.dense(wu[:, :, g * 2048:(g + 1) * 2048], KC, 2048, self.xb, ev_up)

            def ev_dn(c, pst, g=g):
                if g == 0:
                    k.stt(self.xres[:, c, 0:NT], self.xres[:, c, 0:NT], ALPHA, pst, ALU.mult, ALU.add)
                else:
                    k.tt(self.xres[:, c, 0:NT], self.xres[:, c, 0:NT], pst, ALU.add)
            self.dense(wd[:, g * KC:(g + 1) * KC, :], KC, D, hb, ev_dn)

    def ple(self, layer, ps):
        k = self.k
        NT = self.NT
        I = self.I
        psrc = I["p_s"][layer] if self.smp else I["p_p"][layer][ps * NTP:(ps + 1) * NTP, :]
        ntt = (NT + 127) // 128
        tsz = min(NT, 128)
        pT = self.pT
        for tt in range(ntt):
            b = self.io()
            k.dma("sp", b[0:tsz, 0:256], psrc[tt * 128:tt * 128 + tsz, :])
            pst = k.ps()
            for j in range(2):
                self.trans(pst[:, j * 128:j * 128 + tsz], b[0:tsz, j * 128:(j + 1) * 128])
            for j in range(2):
                k.copy(pT[:, j, tt * 128:tt * 128 + tsz], pst[:, j * 128:j * 128 + tsz])
        wpl = I["w_ple"][layer].rearrange("(kc p) n -> p kc n", p=128)
        wg = I["w_ple_gate"][layer].rearrange("(kc p) n -> p kc n", p=128)
        nxb = self.mixb
        for t0 in range(0, D, WT):
            wt = self.wload(wg[:, :, t0:t0 + WT], KC, WT)
            wp = self.wpb[self.wpn % 2].re("p (k n) -> p k n", k=2)
            self.wpn += 1
            k.dma("pool", wp, wpl[:, :, t0:t0 + WT])
            for cc in range(WT // 128):
                c = t0 // 128 + cc
                pst = k.ps()
                for kc in range(KC):
                    k.mm(pst[:, 0:NT], wt[:, kc, cc * 128:(cc + 1) * 128], self.xb[:, kc, 0:NT], kc == 0, kc == KC - 1, sig=(kc == KC - 1))
                gt = self.sc()
                k.act(gt[:, 0:NT], pst[:, 0:NT], AF.Sigmoid)
                p2 = k.ps()
                for j in range(2):
                    k.mm(p2[:, 0:NT], wp[:, j, cc * 128:(cc + 1) * 128], pT[:, j, 0:NT], j == 0, j == 1)
                k.tt(gt[:, 0:NT], gt[:, 0:NT], p2[:, 0:NT], ALU.mult)
                k.tt(self.xres[:, c, 0:NT], self.xres[:, c, 0:NT], gt[:, 0:NT], ALU.add)
                k.copy(nxb[:, c, 0:NT], self.xres[:, c, 0:NT], e="act")
        self.xb, self.mixb = self.mixb, self.xb

    def conv_chunk(self, pst, ch, dest, silu):
        k = self.k
        NT, nseq, T = self.NT, self.nseq, self.T
        U = self.U[self.un % 2]
        self.un += 1
        U3 = U[:, 0:nseq * (T + 3)].re("p (s t) -> p s t", s=nseq)
        if self.smp:
            car = self.carry_s[:, ch, :].re("p (s r) -> p s r", r=3)
        else:
            car = self.carry_p[:, ch, :].re("p (s r) -> p s r", s=1)
        k.copy(U3[:, :, 3:T + 3], pst.re("p (s t) -> p s t", s=nseq), e="act")
        k.copy(U3[:, :, 0:3], car)
        acc = self.sc()
        a3 = acc[:, 0:NT].re("p (s t) -> p s t", s=nseq)
        cwc = lambda j: self.cw[:, j * 32 + ch:j * 32 + ch + 1]
        k.act(a3, U3[:, :, 0:T], AF.Identity, bias=self.pp3[:, ch:ch + 1], scale=cwc(0))
        for j in range(1, 4):
            k.stt(a3, U3[:, :, j:j + T], cwc(j), a3, ALU.mult, ALU.add)
        k.copy(car, U3[:, :, T:T + 3])
        if silu:
            k.act(dest, acc[:, 0:NT], AF.Silu)
        else:
            k.copy(dest, acc[:, 0:NT], e="act")

    def l2norm_fm(self, t, scale):
        k = self.k
        NT = self.NT
        sq = self.sc()
        k.act(sq[:, 0:NT], t, AF.Square)
        pss = k.ps()
        k.mm(pss[:, 0:NT], self.ones, sq[:, 0:NT])
        k.act(sq[:, 0:NT], pss[:, 0:NT], AF.Sqrt, bias=1e-6)
        k.recip(sq[:, 0:NT], sq[:, 0:NT])
        k.stt(t, t, scale, sq[:, 0:NT], ALU.mult, ALU.mult)

    def tok_scalars(self, wv):
        k = self.k
        NT, nch = self.NT, self.nch
        wt = self.wload(wv[:, :, 6144:6160], KC, 16)
        pst = k.ps()
        for kc in range(KC):
            k.mm(pst[0:16, 0:NT], wt[:, kc, 0:16], self.xb[:, kc, 0:NT], kc == 0, kc == KC - 1, sig=(kc == KC - 1))
        abT = self.sc()
        k.copy(abT[0:16, 0:NT], pst[0:16, 0:NT])
        pst2 = k.ps()
        for c in range(nch):
            self.trans(pst2[0:64, c * 16:(c + 1) * 16], abT[0:16, c * 64:(c + 1) * 64])
        k.copy(self.t_ab[0:64, 0:nch * 16], pst2[0:64, 0:nch * 16])
        return self.t_ab[0:64, 0:nch * 16].re("p (c f) -> p c f", f=16)

    def seq_rep(self, src2d, dst, func=None):
        k = self.k
        nch = self.nch
        pst = k.ps()
        if self.smp:
            g2 = self.sm()
            k.tt(g2[0:64, 0:128].re("p (s h) -> p s h", h=8), src2d[:, None, :].bc([64, 16, 8]),
                 self.RM[:, :, None].bc([64, 16, 8]), ALU.mult)
            k.mm(pst[:, 0:128], self.ones[0:64, :], g2[0:64, 0:128])
            n = 128
        else:
            k.mm(pst[:, 0:nch * 8], self.ones[0:64, :], src2d)
            n = nch * 8
        if func is None:
            k.copy(dst[:, 0:n], pst[:, 0:n])
        else:
            k.act(dst[:, 0:n], pst[:, 0:n], func)

    def mixer_e(self):
        k = self.k
        I = self.I
        NT, nch = self.NT, self.nch
        wv = I["w_in_e"].rearrange("(kc p) n -> p kc n", p=128)
        ab3 = self.tok_scalars(wv)
        nb = nch * 8
        g3 = self.t_g[0:64, 0:nb].re("p (c h) -> p c h", h=8)
        b3 = self.t_b[0:64, 0:nb].re("p (c h) -> p c h", h=8)
        z = self.sm()
        z3 = z[0:64, 0:nb].re("p (c h) -> p c h", h=8)
        k.tt(z3, ab3[:, :, 0:8], self.hp[0:64, None, 8:16].bc([64, nch, 8]), ALU.add)
        k.act(z3, z3, AF.Exp)
        k.act(z3, z3, AF.Ln, bias=1.0)
        k.tt(g3, z3, self.hp[0:64, None, 0:8].bc([64, nch, 8]), ALU.mult)
        k.act(b3, ab3[:, :, 8:16], AF.Sigmoid)
        g2 = self.t_g[0:64, 0:nb]
        pst = k.ps()
        k.mm(pst[0:64, 0:nb], self.MUd, g2)
        k.copy(self.t_gc[0:64, 0:nb], pst[0:64, 0:nb])
        k.act(self.t_egc[0:64, 0:nb], pst[0:64, 0:nb], AF.Exp)
        pst = k.ps()
        k.mm(pst[0:64, 0:nb], self.SLm, g2)
        tmp = self.sm()
        k.tt(tmp[0:64, 0:nb], pst[0:64, 0:nb], self.t_gc[0:64, 0:nb], ALU.subtract)
        k.act(self.t_edec[0:64, 0:nb], tmp[0:64, 0:nb], AF.Exp)
        self.seq_rep(g2, self.t_egl, AF.Exp)
        self.stop("ts")
        for gi in range(4):
            qT, kT, vT, oT = [self.G[i][:, 0:2 * NT].re("p (j n) -> p j n", j=2) for i in range(4)]
            for (col0, dst, chb, kind) in ((gi * 256, qT, gi * 2, "q"), (1024 + gi * 256, kT, 8 + gi * 2, "k"),
                                          (2048 + gi * 256, vT, 16 + gi * 2, "v")):
                wt = self.wload(wv[:, :, col0:col0 + 256], KC, 256)
                for cc in range(2):
                    pst = k.ps()
                    for kc in range(KC):
                        k.mm(pst[:, 0:NT], wt[:, kc, cc * 128:(cc + 1) * 128], self.xb[:, kc, 0:NT], kc == 0, kc == KC - 1, sig=(kc == KC - 1))
                    self.conv_chunk(pst[:, 0:NT], chb + cc, dst[:, cc, :], True)
                    if kind == "q":
                        self.l2norm_fm(dst[:, cc, :], 128.0 ** -0.5)
                    elif kind == "k":
                        self.l2norm_fm(dst[:, cc, :], 1.0)
            self.stop("g0")
            for c in range(nch):
                self.delta_chunk(gi, c, qT, kT, vT, oT)
                self.stop("c0")
            wt = self.wload(wv[:, :, 4096 + gi * 256:4096 + gi * 256 + 256], KC, 256)
            for j in range(2):
                h = gi * 2 + j
                pst = k.ps()
                for kc in range(KC):
                    k.mm(pst[:, 0:NT], wt[:, kc, j * 128:(j + 1) * 128], self.xb[:, kc, 0:NT], kc == 0, kc == KC - 1, sig=(kc == KC - 1))
                zs = self.sc()
                k.act(zs[:, 0:NT], pst[:, 0:NT], AF.Silu)
                sq = self.sc()
                k.act(sq[:, 0:NT], oT[:, j, :], AF.Square)
                pss = k.ps()
                k.mm(pss[:, 0:NT], self.ones, sq[:, 0:NT])
                k.act(sq[:, 0:NT], pss[:, 0:NT], AF.Sqrt, bias=RMS_EPS, scale=1.0 / 128)
                k.recip(sq[:, 0:NT], sq[:, 0:NT])
                k.tt(sq[:, 0:NT], sq[:, 0:NT], oT[:, j, :], ALU.mult)
                k.stt(self.mixb[:, h, 0:NT], sq[:, 0:NT], self.pp3[:, 56:57], zs[:, 0:NT], ALU.mult, ALU.mult)
        for li in range(4):
            xr, gg = [self.G[i][:, 0:2 * NT].re("p (j n) -> p j n", j=2) for i in range(2)]
            wt = self.wload(wv[:, :, 3072 + li * 256:3072 + li * 256 + 256], KC, 256)
            for cc in range(2):
                pst = k.ps()
                for kc in range(KC):
                    k.mm(pst[:, 0:NT], wt[:, kc, cc * 128:(cc + 1) * 128], self.xb[:, kc, 0:NT], kc == 0, kc == KC - 1, sig=(kc == KC - 1))
                self.conv_chunk(pst[:, 0:NT], 24 + li * 2 + cc, xr[:, cc, :], False)
                k.copy(self.xrb[:, cc, 0:NT], xr[:, cc, :], e="act")
            wt = self.wload(wv[:, :, 5120 + li * 256:5120 + li * 256 + 256], KC, 256)
            for cc in range(2):
                pst = k.ps()
                for kc in range(KC):
                    k.mm(pst[:, 0:NT], wt[:, kc, cc * 128:(cc + 1) * 128], self.xb[:, kc, 0:NT], kc == 0, kc == KC - 1, sig=(kc == KC - 1))
                x0 = self.sc()
                k.copy(x0[:, 0:NT], pst[:, 0:NT], e="act")
                t = self.sc()
                k.tt(t[:, 0:NT], x0[:, 0:NT], x0[:, 0:NT], ALU.mult)
                k.ts(t[:, 0:NT], t[:, 0:NT], 0.044715, 1.0, ALU.mult, ALU.add)
                k.tt(t[:, 0:NT], t[:, 0:NT], x0[:, 0:NT], ALU.mult)
                k.act(t[:, 0:NT], t[:, 0:NT], AF.Sigmoid, scale=1.5957691216057308)
                k.tt(gg[:, cc, :], t[:, 0:NT], x0[:, 0:NT], ALU.mult)
            for cc in range(2):
                blk = li * 2 + cc
                pr = k.ps()
                k.mm(pr[:, 0:NT], self.wrb[:, blk * 128:(blk + 1) * 128], self.xrb[:, cc, 0:NT])
                ra = self.sc()
                k.act(ra[:, 0:NT], pr[:, 0:NT], AF.Sigmoid, bias=self.pp3[:, 32 + blk:33 + blk])
                pi = k.ps()
                k.mm(pi[:, 0:NT], self.wib[:, blk * 128:(blk + 1) * 128], self.xrb[:, cc, 0:NT])
                ia = self.sc()
                k.act(ia[:, 0:NT], pi[:, 0:NT], AF.Sigmoid, bias=self.pp3[:, 40 + blk:41 + blk])
                a = self.sc()
                k.act(a[:, 0:NT], ra[:, 0:NT], AF.Exp, scale=self.nsl[:, blk:blk + 1])
                bx = self.sc()
                k.act(bx[:, 0:NT], ra[:, 0:NT], AF.Exp, scale=self.nsl[:, 8 + blk:9 + blk])
                k.ts(bx[:, 0:NT], bx[:, 0:NT], -1.0, 1.0, ALU.mult, ALU.add)
                k.act(bx[:, 0:NT], bx[:, 0:NT], AF.Sqrt)
                k.tt(bx[:, 0:NT], bx[:, 0:NT], ia[:, 0:NT], ALU.mult)
                k.tt(bx[:, 0:NT], bx[:, 0:NT], xr[:, cc, :], ALU.mult)
                hT = self.sc()
                if not self.smp:
                    k.scan(hT[:, 0:NT], a[:, 0:NT], bx[:, 0:NT], self.hl[:, blk, 0:1])
                    k.copy(self.hl[:, blk, 0:1], hT[:, NT - 1:NT])
                else:
                    h3 = hT[:, 0:NT].re("p (s t) -> p s t", t=4)
                    a3 = a[:, 0:NT].re("p (s t) -> p s t", t=4)
                    b3_ = bx[:, 0:NT].re("p (s t) -> p s t", t=4)
                    for t_ in range(4):
                        prev = self.hl[:, blk, :] if t_ == 0 else h3[:, :, t_ - 1]
                        k.tt(h3[:, :, t_], a3[:, :, t_], prev, ALU.mult)
                        k.tt(h3[:, :, t_], h3[:, :, t_], b3_[:, :, t_], ALU.add)
                    k.copy(self.hl[:, blk, :], h3[:, :, 3])
                k.tt(self.mixb[:, 8 + blk, 0:NT], hT[:, 0:NT], gg[:, cc, :], ALU.mult)

    def delta_chunk(self, gi, c, qT, kT, vT, oT):
        k = self.k
        nseq = self.nseq
        ct = self.ct
        cs = slice(c * 64, (c + 1) * 64)
        n0 = c * 8 + gi * 2
        qc, kc_, vc = qT[:, :, cs], kT[:, :, cs], vT[:, :, cs]
        bc3 = lambda t, w: t[0:64, n0:n0 + 2][:, :, None].bc([64, 2, w])
        m3 = lambda m, w=64: m[:, None, :].bc([64, 2, w])
        v3 = lambda t, p=64, w=64: t[0:p, 0:2 * w].re("p (j i) -> p j i", j=2)
        dg = self.sm()
        k.tt(v3(dg), m3(self.MUd), bc3(self.t_g, 64), ALU.mult)
        pgr = k.ps()
        k.mm(pgr[:, 0:128], self.ones[0:64, :], dg[0:64, 0:128])
        k.act(ct["egrow"][:, 0:128], pgr[:, 0:128], AF.Exp)
        a1 = self.sm()
        k.tt(v3(a1), pgr[0:64, 0:128].re("p (j i) -> p j i", j=2), bc3(self.t_gc, 64), ALU.subtract)
        k.tt(v3(a1), v3(a1), m3(self.NEGm), ALU.add)
        k.act(ct["dmd"][0:64, 0:128], a1[0:64, 0:128], AF.Exp)
        k.tt(v3(ct["bm"]), m3(self.MU), bc3(self.t_b, 64), ALU.mult)
        pkk = k.ps()
        pqk = k.ps()
        for j in range(2):
            k.mm(pkk[0:64, j * 64:(j + 1) * 64], kc_[:, j, :], kc_[:, j, :], sig=(j == 1))
        for j in range(2):
            k.mm(pqk[0:64, j * 64:(j + 1) * 64], kc_[:, j, :], qc[:, j, :], sig=(j == 1))
        N0 = ct["n0"]
        k.tt(N0[0:64, 0:128], pkk[0:64, 0:128], ct["dmd"][0:64, 0:128], ALU.mult)
        k.tt(N0[0:64, 0:128], N0[0:64, 0:128], ct["bm"][0:64, 0:128], ALU.mult)
        k.tt(ct["qkm"][0:64, 0:128], pqk[0:64, 0:128], ct["dmd"][0:64, 0:128], ALU.mult)
        pnt = k.ps()
        for j in range(2):
            self.k.mm(pnt[0:64, j * 64:(j + 1) * 64], N0[0:64, j * 64:(j + 1) * 64], self.I64, sig=(j == 1))
        k.copy(ct["n0t"][0:64, 0:128], pnt[0:64, 0:128], e="act")
        X = self.npl()
        k.stt(v3(X), v3(N0), -1.0, m3(self.I64), ALU.mult, ALU.add)
        A, AT = N0, ct["n0t"]
        js = lambda t, j: t[0:64, j * 64:(j + 1) * 64]
        for lvl in range(1, 6):
            if lvl < 5:
                pa = k.ps()
                for j in range(2):
                    k.mm(js(pa, j), js(AT, j), js(A, j), sig=(j == 1))
            pat = k.ps()
            for j in range(2):
                k.mm(js(pat, j), js(A, j), js(AT, j), sig=(j == 1))
            ATn = self.npl()
            BT = self.npl()
            k.copy(ATn[0:64, 0:128], pat[0:64, 0:128], e="act")
            k.tt(v3(BT), pat[0:64, 0:128].re("p (j i) -> p j i", j=2), m3(self.I64), ALU.add)
            if lvl < 5:
                An = self.npl()
                k.copy(An[0:64, 0:128], pa[0:64, 0:128])
            px = k.ps()
            for j in range(2):
                k.mm(js(px, j), js(BT, j), js(X, j), sig=(j == 1))
            Xn = self.npl()
            k.copy(Xn[0:64, 0:128], px[0:64, 0:128], e="act")
            if lvl < 5:
                A = An
            AT, X = ATn, Xn
        YT = X
        pkt = k.ps()
        for j in range(2):
            k.mm(pkt[0:64, j * 128:(j + 1) * 128], kc_[:, j, :], self.ident(128), sig=(j == 1))
        k.tt(v3(ct["kg"], 64, 128), pkt[0:64, 0:256].re("p (j i) -> p j i", j=2), bc3(self.t_egc, 128), ALU.mult)
        k.tt(v3(ct["kdec"], 64, 128), pkt[0:64, 0:256].re("p (j i) -> p j i", j=2), bc3(self.t_edec, 128), ALU.mult)
        pvt = k.ps()
        for j in range(2):
            k.mm(pvt[0:64, j * 128:(j + 1) * 128], vc[:, j, :], self.ident(128), sig=(j == 1))
        k.copy(ct["vtm"][0:64, 0:256], pvt[0:64, 0:256], e="act")
        pw = k.ps()
        for j in range(2):
            k.mm(pw[:, j * 64:(j + 1) * 64], ct["kg"][0:64, j * 128:(j + 1) * 128], js(YT, j), sig=(j == 1))
        k.act(ct["wnt"][:, 0:128], pw[:, 0:128], AF.Copy, scale=-1.0)
        k.tt(v3(ct["qgt"], 128), qc, v3(ct["egrow"], 128), ALU.mult)
        for j in range(2):
            h = gi * 2 + j
            if self.smp:
                Sb = self.S_s
                k.dma("act", Sb, self.I["st_delta"][:, h].rearrange("s k v -> k s v"))
                Sv = lambda s: Sb[:, s, :]
                eg = lambda s: self.t_egl[:, s * 8 + h:s * 8 + h + 1]
            else:
                Sv = lambda s: self.S_p[:, h, :]
                eg = lambda s: self.t_egl[:, c * 8 + h:c * 8 + h + 1]
            wn_j = ct["wnt"][:, j * 64:(j + 1) * 64]
            qg_j = ct["qgt"][:, j * 64:(j + 1) * 64]
            if self.smp:
                wm = self.msk[:, 0:1024].re("p (s i) -> p s i", s=16)
                k.tt(wm, wn_j[:, None, :].bc([128, 16, 64]), self.SM3, ALU.mult)
            pv = k.ps()
            k.mm(pv[0:64, 0:128], js(YT, j), ct["vtm"][0:64, j * 128:(j + 1) * 128], True, False, sig=False)
            for s in range(nseq):
                lhs = wm[:, s, :] if self.smp else wn_j
                k.mm(pv[0:64, 0:128], lhs, Sv(s), False, s == nseq - 1)
            vn_j = ct["vnew"][0:64, j * 128:(j + 1) * 128]
            k.ts(vn_j, pv[0:64, 0:128], self.t_b[0:64, n0 + j:n0 + j + 1], None, ALU.mult)
            po = k.ps()
            k.mm(po[:, 0:64], vn_j, js(ct["qkm"], j), True, False, sig=False)
            for s in range(nseq):
                cols = slice(s * 4, s * 4 + 4) if self.smp else slice(0, 64)
                k.mm(po[:, cols], Sv(s), qg_j[:, cols], False, s == nseq - 1, sig=(s == nseq - 1))
            k.copy(oT[:, j, cs], po[:, 0:64], e="act")
            kd_j = ct["kdec"][0:64, j * 128:(j + 1) * 128]
            for half in range(2 if self.smp else 1):
                if self.smp:
                    km = self.msk[0:64, 0:1024].re("p (s i) -> p s i", s=8)
                    k.tt(km, kd_j[:, None, :].bc([64, 8, 128]),
                         self.RM[:, half * 8:(half + 1) * 8][:, :, None].bc([64, 8, 128]), ALU.mult)
                for s8 in range(8 if self.smp else 1):
                    s = half * 8 + s8
                    lhs = km[:, s8, :] if self.smp else kd_j
                    pss = k.ps()
                    k.mm(pss[:, 0:128], lhs, vn_j)
                    k.stt(Sv(s), Sv(s), eg(s), pss[:, 0:128], ALU.mult, ALU.add)
            if self.smp:
                k.dma("act", self.O["delta_s"][:, h].rearrange("s k v -> k s v"), Sb)

    def mixer_o(self):
        k = self.k
        I = self.I
        NT, nch = self.NT, self.nch
        wv = I["w_in_o"].rearrange("(kc p) n -> p kc n", p=128)
        ab3 = self.tok_scalars(wv)
        nb = nch * 8
        ig3 = self.t_g[0:64, 0:nb].re("p (c h) -> p c h", h=8)
        lf3 = self.t_b[0:64, 0:nb].re("p (c h) -> p c h", h=8)
        k.tt(ig3, ab3[:, :, 0:8], self.hp[0:64, None, 16:24].bc([64, nch, 8]), ALU.add)
        k.tt(lf3, ab3[:, :, 8:16], self.hp[0:64, None, 24:32].bc([64, nch, 8]), ALU.add)
        k.act(lf3, lf3, AF.Exp, scale=-1.0)
        k.act(lf3, lf3, AF.Ln, bias=1.0)
        lf2 = self.t_b[0:64, 0:nb]
        k.ts(lf2, lf2, -1.0)
        pst = k.ps()
        k.mm(pst[0:64, 0:nb], self.MUd, lf2)
        k.copy(self.t_gc[0:64, 0:nb], pst[0:64, 0:nb])
        pst = k.ps()
        k.mm(pst[0:64, 0:nb], self.SLm, lf2)
        k.tt(self.t_gend[0:64, 0:nb], pst[0:64, 0:nb], self.t_gc[0:64, 0:nb], ALU.subtract)
        k.tt(self.t_gend[0:64, 0:nb], self.t_gend[0:64, 0:nb], self.t_g[0:64, 0:nb], ALU.add)
        self.seq_rep(lf2, self.t_egl, None)
        for pi in range(4):
            qT, kT = [self.G[i][:, 0:2 * NT].re("p (j n) -> p j n", j=2) for i in range(2)]
            ktm = V(self.GG[0:64, 2048:2048 + nch * 256], [("G", 2), ("G", 3)]).re("p (c n) -> p c n", n=256)
            wt = self.wload(wv[:, :, pi * 256:pi * 256 + 256], KC, 256)
            for cc in range(2):
                pst = k.ps()
                for kc in range(KC):
                    k.mm(pst[:, 0:NT], wt[:, kc, cc * 128:(cc + 1) * 128], self.xb[:, kc, 0:NT], kc == 0, kc == KC - 1, sig=(kc == KC - 1))
                k.copy(qT[:, cc, :], pst[:, 0:NT], e="act")
            wt = self.wload(wv[:, :, 1024 + pi * 256:1024 + pi * 256 + 256], KC, 256)
            for cc in range(2):
                pst = k.ps()
                for kc in range(KC):
                    k.mm(pst[:, 0:NT], wt[:, kc, cc * 128:(cc + 1) * 128], self.xb[:, kc, 0:NT], kc == 0, kc == KC - 1, sig=(kc == KC - 1))
                k.act(kT[:, cc, :], pst[:, 0:NT], AF.Copy, scale=128.0 ** -0.5)
            for c in range(nch):
                pst = k.ps()
                for kc in range(KC):
                    k.mm(pst[0:64, 0:256], self.xb[:, kc, c * 64:(c + 1) * 64], wt[:, kc, :], kc == 0, kc == KC - 1, sig=(kc == KC - 1))
                k.ts(ktm[:, c, :], pst[0:64, 0:256], 128.0 ** -0.5)
            for j in range(2):
                h = pi * 2 + j
                wvt = self.wload(wv[:, :, 2048 + h * 256:2048 + h * 256 + 256], KC, 256)
                wot = self.wload(wv[:, :, 4096 + h * 256:4096 + h * 256 + 256], KC, 256)
                if self.smp:
                    self.load_cn_sample(h)
                self.mlstm_proj(0, wvt, wot)
                for c in range(nch):
                    if c + 1 < nch:
                        self.mlstm_proj(c + 1, wvt, wot)
                    self.mlstm_chunk(h, j, c, qT, kT, ktm, wvt, wot)
                if self.smp:
                    self.store_cn_sample(h)

    def mlstm_proj(self, c, wvt, wot):
        k = self.k
        cs = slice(c * 64, (c + 1) * 64)
        vps = k.ps()
        for kc in range(KC):
            k.mm(vps[0:64, 0:256], self.xb[:, kc, cs], wvt[:, kc, :], kc == 0, kc == KC - 1, sig=(kc == KC - 1))
        vaug = (self.vaug_t, self.vaug_t2)[c % 2][0:64, 0:257]
        k.copy(vaug[:, 0:256], vps[0:64, 0:256], e="act")
        ops = k.ps()
        for kc in range(KC):
            k.mm(ops[0:64, 0:256], self.xb[:, kc, cs], wot[:, kc, :], kc == 0, kc == KC - 1, sig=(kc == KC - 1))
        osig = (self.ct["osig"], self.osig2)[c % 2][0:64, 0:256]
        k.act(osig, ops[0:64, 0:256], AF.Sigmoid)

    def mlstm_chunk(self, h, j, c, qT, kT, ktm, wvt, wot):
        k = self.k
        nseq = self.nseq
        ct = self.ct
        cs = slice(c * 64, (c + 1) * 64)
        n = c * 8 + h
        col = lambda t: t[0:64, n:n + 1]
        vaug = (self.vaug_t, self.vaug_t2)[c % 2][0:64, 0:257]
        osig = (ct["osig"], self.osig2)[c % 2][0:64, 0:256]
        dl = self.sm()
        k.ts(dl[0:64, 0:64], self.MUd, col(self.t_b), None, ALU.mult)
        k.stt(dl[0:64, 0:64], self.I64, col(self.t_g), dl[0:64, 0:64], ALU.mult, ALU.subtract)
        k.ts(dl[0:64, 64:128], self.SLm, col(self.t_b), None, ALU.mult)
        pr = k.ps()
        k.mm(pr[:, 0:128], self.ones[0:64, :], dl[0:64, 0:128])
        rn = ct["rn"]
        k.copy(rn[:, 0:128], pr[:, 0:128])
        dmat = self.sm()
        k.ts(dmat[0:64, 0:64], rn[0:64, 0:64], col(self.t_gc), None, ALU.add)
        k.tt(dmat[0:64, 0:64], dmat[0:64, 0:64], self.NEGmT, ALU.add)
        sv = self.sm()
        rmx, mtok, inter, mt, e_, emt, dn, ssum, wcol, mnt = [sv[0:64, i:i + 1] for i in range(10)]
        k.reduce(rmx, dmat[0:64, 0:64], ALU.max)
        if self.smp:
            m3 = self.mst[:, 0:128].re("p (s h) -> p s h", h=8)[:, :, h]
            gl3 = self.t_egl[:, 0:128].re("p (s h) -> p s h", h=8)[:, :, h]
            tmp = self.sm()
            k.tt(tmp[0:64, 0:16], m3[0:64, :], self.RM, ALU.mult)
            k.reduce(mtok, tmp[0:64, 0:16], ALU.add)
        else:
            m3 = self.mst[:, h:h + 1]
            gl3 = self.t_egl[:, n:n + 1]
            k.copy(mtok, m3[0:64, :])
        k.tt(inter, col(self.t_gc), mtok, ALU.add)
        k.tt(mt, inter, rmx, ALU.max)
        k.tt(e_, inter, mt, ALU.subtract)
        k.act(e_, e_, AF.Exp)
        k.act(emt, mt, AF.Exp, scale=-1.0)
        k.ts(dmat[0:64, 0:64], dmat[0:64, 0:64], mt, None, ALU.subtract)
        k.act(dmat[0:64, 0:64], dmat[0:64, 0:64], AF.Exp)
        pqk = k.ps()
        k.mm(pqk[0:64, 0:64], qT[:, j, cs], kT[:, j, cs])
        smat = self.sm()
        k.tt(smat[0:64, 0:64], pqk[0:64, 0:64], dmat[0:64, 0:64], ALU.mult)
        pst_ = k.ps()
        k.mm(pst_[0:64, 0:64], smat[0:64, 0:64], self.I64)
        sT = self.sm()
        k.copy(sT[0:64, 0:64], pst_[0:64, 0:64], e="act")
        if self.smp:
            Cv = lambda s: self.Cn_s[:, s, :]
            qm = self.msk[:, 0:1024].re("p (s i) -> p s i", s=16)
            k.tt(qm, qT[:, j, cs][:, None, :].bc([128, 16, 64]), self.SM3, ALU.mult)
        else:
            Cv = lambda s: self.Cn_p[:, h, :]
        p1 = k.ps()
        for s in range(nseq):
            lhs = qm[:, s, :] if self.smp else qT[:, j, cs]
            k.mm(p1[0:64, 0:257], lhs, Cv(s), s == 0, s == nseq - 1)
        p2 = k.ps()
        k.mm(p2[0:64, 0:257], sT[0:64, 0:64], vaug)
        nd = self.nd_t[0:64, 0:257]
        k.ts(nd, p1[0:64, 0:257], e_, None, ALU.mult)
        k.tt(nd, nd, p2[0:64, 0:257], ALU.add)
        k.act(dn, nd[:, 256:257], AF.Abs)
        k.tt(dn, dn, emt, ALU.max)
        k.recip(dn, dn)
        hr = ct["hr"][0:64, 0:256]
        k.ts(hr, nd[:, 0:256], dn, None, ALU.mult)
        junk = ct["yt"][0:64, 0:256]
        k.tt(junk, hr, hr, ALU.mult)
        k.reduce(ssum, junk, ALU.add)
        k.act(ssum, ssum, AF.Sqrt, bias=RMS_EPS, scale=1.0 / 256)
        k.recip(ssum, ssum)
        yt = ct["yt"][0:64, 0:256]
        k.stt(yt, hr, ssum, osig, ALU.mult, ALU.mult)
        for hf in range(2):
            pt = k.ps()
            k.mm(pt[:, 0:64], yt[:, hf * 128:(hf + 1) * 128], self.I64)
            k.ts(self.mixb[:, h * 2 + hf, cs], pt[:, 0:64], self.pp3[:, 57 + h * 2 + hf:58 + h * 2 + hf], None, ALU.mult)
        gr = self.sm()
        k.tt(gr[:, 0:64], rn[:, 0:64], rn[:, 64:128], ALU.add)
        mx = self.sm()
        T_ = 64 // nseq
        k.reduce(mx[:, 0:nseq], gr[:, 0:64].re("p (s t) -> p s t", t=T_), ALU.max)
        t1 = self.sm()
        k.tt(t1[:, 0:nseq], gl3, m3, ALU.add)
        mnew = self.sm()
        k.tt(mnew[:, 0:nseq], t1[:, 0:nseq], mx[:, 0:nseq], ALU.max)
        scf = self.sm()
        k.tt(scf[:, 0:nseq], t1[:, 0:nseq], mnew[:, 0:nseq], ALU.subtract)
        k.act(scf[:, 0:nseq], scf[:, 0:nseq], AF.Exp)
        if self.smp:
            tmp = self.sm()
            k.tt(tmp[0:64, 0:16], mnew[0:64, 0:16], self.RM, ALU.mult)
            k.reduce(mnt, tmp[0:64, 0:16], ALU.add)
        else:
            k.copy(mnt, mnew[0:64, 0:1])
        k.tt(wcol, col(self.t_gend), mnt, ALU.subtract)
        k.act(wcol, wcol, AF.Exp)
        vw = self.vw_t[0:64, 0:257]
        k.ts(vw, vaug, wcol, None, ALU.mult)
        kt_j = ktm[:, c, j * 128:(j + 1) * 128]
        for half in range(2 if self.smp else 1):
            if self.smp:
                km = self.msk[0:64, 0:1024].re("p (s i) -> p s i", s=8)
                k.tt(km, kt_j[:, None, :].bc([64, 8, 128]),
                     self.RM[:, half * 8:(half + 1) * 8][:, :, None].bc([64, 8, 128]), ALU.mult)
            for s8 in range(8 if self.smp else 1):
                s = half * 8 + s8
                lhs = km[:, s8, :] if self.smp else kt_j
                p3 = k.ps()
                k.mm(p3[:, 0:257], lhs, vw)
                k.stt(Cv(s), Cv(s), scf[:, s:s + 1], p3[:, 0:257], ALU.mult, ALU.add)
        k.copy(m3, mnew[:, 0:nseq])

    def load_sample_states(self):
        k = self.k
        I = self.I
        allx = [("xres", i) for i in range(KC)]
        self.S_s = V(self.xres.t[:, :, 64:192], [("S_s", 0)])
        self.Cn_s = V(self.xres.t[:, :, 192:449], [("Cn_s", 0)])
        for nk in (("S_s", 0), ("Cn_s", 0)):
            rd = {}
            for ok in allx:
                w = k.lastw.get(ok)
                for tok in ([w] if w else []) + list(k.readers.get(ok, {}).values()):
                    o = rd.get(tok[0].name)
                    if o is None or o[1] < tok[1]:
                        rd[tok[0].name] = tok
            k.readers[nk] = rd
        scv = I["st_conv"]
        for pc in range(4):
            b = self.io()
            k.dma("sp", b[0:48, :], scv[:, pc * 1024:(pc + 1) * 1024])
            for c4 in range(2):
                pst = k.ps()
                for j in range(4):
                    self.trans(pst[:, j * 48:(j + 1) * 48], b[0:48, (c4 * 4 + j) * 128:(c4 * 4 + j + 1) * 128])
                for j in range(4):
                    k.copy(self.carry_s[:, pc * 8 + c4 * 4 + j, :], pst[:, j * 48:(j + 1) * 48])
        b = self.io()
        k.dma("sp", b[0:16, :], I["st_lru"])
        pst = k.ps()
        for blk in range(8):
            self.trans(pst[:, blk * 16:(blk + 1) * 16], b[0:16, blk * 128:(blk + 1) * 128])
        k.copy(self.hl[:, :, :], pst[:, 0:128].re("p (b s) -> p b s", s=16))
        k.dma("sp", self.mst, I["st_mm"].rearrange("s h -> (s h)").rearrange("(o n) -> o n", o=1).to_broadcast([128, 128]))

    def load_cn_sample(self, h):
        k = self.k
        I = self.I
        for s in range(16):
            stg = self.sc()
            k.dma("act", stg[:, 0:256].re("p (a k) -> p a k", a=2), I["st_mc"][s, h].rearrange("(a p) k -> p a k", p=128))
            pst = k.ps()
            for a in range(2):
                self.trans(pst[:, a * 128:(a + 1) * 128], stg[:, a * 128:(a + 1) * 128])
            k.copy(self.Cn_s[:, s, 0:256], pst[:, 0:256])
        b = self.sc()
        k.dma("act", b[0:16, 0:128], I["st_mn"][:, h, :])
        pst = k.ps()
        self.trans(pst[:, 0:16], b[0:16, 0:128])
        k.copy(self.Cn_s[:, :, 256], pst[:, 0:16])

    def store_cn_sample(self, h):
        k = self.k
        O = self.O
        for s in range(16):
            pst = k.ps()
            for a in range(2):
                self.trans(pst[:, a * 128:(a + 1) * 128], self.Cn_s[:, s, a * 128:(a + 1) * 128])
            stg = self.sc()
            k.copy(stg[:, 0:256], pst[:, 0:256], e="act")
            k.dma("act", O["mc_s"][s, h].rearrange("(a p) k -> p a k", p=128), stg[:, 0:256].re("p (a k) -> p a k", a=2))
        pst = k.ps()
        self.trans(pst[0:16, 0:128], self.Cn_s[:, :, 256])
        b = self.sc()
        k.copy(b[0:16, 0:128], pst[0:16, 0:128])
        k.dma("act", O["mn_s"][:, h, :], b[0:16, 0:128])

    def store_prompt_states(self):
        k = self.k
        O = self.O
        k.dma("sp", O["delta_p"].rearrange("h k v -> k h v"), self.S_p[:, :, :])
        stg = self.io()
        pst = k.ps()
        for r in range(3):
            self.trans(pst[0:32, r * 128:(r + 1) * 128], self.carry_p[:, :, r])
        k.copy(stg[0:32, 0:384], pst[0:32, 0:384])
        k.dma("sp", O["conv_p"].rearrange("r (c p) -> c r p", p=128), stg[0:32, 0:384].re("c (r p) -> c r p", r=3))
        pst = k.ps()
        self.trans(pst[0:8, 0:128], self.hl[:, :, 0])
        k.copy(stg[0:8, 384:512], pst[0:8, 0:128])
        k.dma("sp", O["lru_p"], stg[0:8, 384:512])
        for h in range(8):
            pst = k.ps()
            for a in range(2):
                self.trans(pst[:, a * 128:(a + 1) * 128], self.Cn_p[:, h, a * 128:(a + 1) * 128])
            s2 = self.sc()
            k.copy(s2[:, 0:256], pst[:, 0:256], e="act")
            k.dma("sp", O["mc_p"][h].rearrange("(a p) k -> p a k", p=128), s2[:, 0:256].re("p (a k) -> p a k", a=2))
        pst = k.ps()
        self.trans(pst[0:8, 0:128], self.Cn_p[:, :, 256])
        k.copy(stg[0:8, 512:640], pst[0:8, 0:128])
        k.dma("sp", O["mn_p"], stg[0:8, 512:640])
        k.dma("sp", O["mm_p"], self.mst[0:1, 0:8])

    def store_sample_states(self):
        k = self.k
        O = self.O
        cso = O["conv_s"].rearrange("s r n -> (s r) n")
        for pc in range(4):
            b = self.io()
            for c4 in range(2):
                pst = k.ps()
                for j in range(4):
                    self.trans(pst[0:48, j * 128:(j + 1) * 128], self.carry_s[:, pc * 8 + c4 * 4 + j, :])
                k.copy(b[0:48, c4 * 512:(c4 + 1) * 512], pst[0:48, :])
            k.dma("sp", cso[:, pc * 1024:(pc + 1) * 1024], b[0:48, :])
        b = self.io()
        pst = k.ps()
        pst2 = k.ps()
        for blk in range(8):
            p_ = pst if blk < 4 else pst2
            self.trans(p_[0:16, (blk % 4) * 128:(blk % 4 + 1) * 128], self.hl[:, blk, :])
        k.copy(b[0:16, 0:512], pst[0:16, :])
        k.copy(b[0:16, 512:1024], pst2[0:16, :])
        k.dma("sp", O["lru_s"], b[0:16, :])
        k.dma("sp", O["mm_s"].rearrange("s h -> (s h)").rearrange("(o n) -> o n", o=1), self.mst[0:1, 0:128])


IN_SPECS = [("x_p", (2048, D)), ("x_s", (64, D)), ("p_p", (2, 2048, 256)), ("p_s", (2, 64, 256)),
            ("st_conv", (48, 4096)), ("st_delta", (16, 8, 128, 128)), ("st_lru", (16, 1024)),
            ("st_mc", (16, 8, 256, 128)), ("st_mn", (16, 8, 128)), ("st_mm", (16, 8)),
            ("w_in_e", (D, 6160)), ("w_conv_e", (4, 4096)), ("b_conv_e", (1, 4096)), ("a_log_e", (1, 8)),
            ("dt_bias_e", (1, 8)), ("delta_norm_e", (1, 128)), ("lru_wr_e", (8, 128, 128)),
            ("lru_br_e", (1, 1024)), ("lru_wi_e", (8, 128, 128)), ("lru_bi_e", (1, 1024)),
            ("lru_lambda_e", (1, 1024)), ("w_out_e", (D, D)), ("w_in_o", (D, 6160)), ("b_ig_o", (1, 8)),
            ("b_fg_o", (1, 8)), ("mlstm_norm_o", (1, 2048)), ("w_out_o", (D, D)),
            ("ln1_g", (2, D)), ("ln1_b", (2, D)), ("ln2_g", (2, D)), ("ln2_b", (2, D)),
            ("w_up", (2, D, 8192)), ("w_down", (2, 8192, D)), ("w_ple", (2, 256, D)),
            ("w_ple_gate", (2, D, D)), ("consts", (128, C_END))]
OUT_SPECS = [("y_p", (2048, D)), ("y_s", (64, D)), ("conv_p", (3, 4096)), ("delta_p", (8, 128, 128)),
             ("lru_p", (8, 128)), ("mc_p", (8, 256, 128)), ("mn_p", (8, 128)), ("mm_p", (1, 8)),
             ("conv_s", (16, 3, 4096)), ("delta_s", (16, 8, 128, 128)), ("lru_s", (16, 1024)),
             ("mc_s", (16, 8, 256, 128)), ("mn_s", (16, 8, 128)), ("mm_s", (16, 8))]


def make_consts():
    c = np.zeros((128, C_END), np.float32)
    c[:, C_ID:C_ID + 128] = np.eye(128, dtype=np.float32)
    m = np.arange(64)
    for base, seqlen in ((C_P, 64), (C_S, 4)):
        same = (m[:, None] // seqlen) == (m[None, :] // seqlen)
        mud = same & (m[:, None] <= m[None, :])
        mu = same & (m[:, None] < m[None, :])
        c[:64, base:base + 64] = mud
        c[:64, base + 64:base + 128] = mu
        c[:64, base + 128:base + 192] = same
        c[:64, base + 192:base + 256] = np.where(mud, 0.0, NEG)
        nt0 = C_NTP if base == C_P else C_NTS
        c[:64, nt0:nt0 + 64] = np.where(mud.T, 0.0, NEG)
    c[:64, C_RM:C_RM + 16] = (m[:, None] // 4) == np.arange(16)[None, :]
    sm = (np.arange(16)[:, None] == (m[None, :] // 4)).astype(np.float32).reshape(1, 1024)
    c[:, C_SM:C_SM + 1024] = sm
    return c


_CACHE = {}


def kernel(**inp):
    f = lambda a: np.ascontiguousarray(np.asarray(a, dtype=np.float32))
    if "prog" not in _CACHE:
        _CACHE["prog"] = Prog()
    prog = _CACHE["prog"]
    consts = make_consts()
    shared = {}
    for n in ("w_in_e", "w_conv_e", "b_conv_e", "a_log_e", "dt_bias_e", "delta_norm_e", "lru_wr_e", "lru_br_e",
              "lru_wi_e", "lru_bi_e", "lru_lambda_e", "w_out_e", "w_in_o", "b_ig_o", "b_fg_o", "mlstm_norm_o", "w_out_o"):
        a = f(inp[n])[0]
        if a.ndim == 1:
            a = a[None]
        shared[n] = np.ascontiguousarray(a)
    for n in ("ln1_g", "ln1_b", "ln2_g", "ln2_b", "w_up", "w_down", "w_ple", "w_ple_gate"):
        shared[n] = f(inp[n])
    shared["consts"] = consts
    in_maps = []
    for c in range(8):
        b = c % 4
        sl = slice(16 * c, 16 * c + 16)
        m = dict(shared)
        m["x_p"] = f(inp["x_prompt"][b])
        m["x_s"] = f(inp["x_sample"][sl]).reshape(64, D)
        m["p_p"] = f(inp["p_prompt"][:, b])
        m["p_s"] = f(inp["p_sample"][:, sl]).reshape(2, 64, 256)
        m["st_conv"] = f(inp["state_conv"][0, sl]).reshape(48, 4096)
        m["st_delta"] = f(inp["state_delta"][0, sl])
        m["st_lru"] = f(inp["state_lru"][0, sl])
        m["st_mc"] = f(inp["state_mlstm_c"][0, sl])
        m["st_mn"] = f(inp["state_mlstm_n"][0, sl])
        m["st_mm"] = f(inp["state_mlstm_m"][0, sl])
        in_maps.append(m)
    import os
    ncores = int(os.environ.get("KCORES", "8"))
    res = run_bass_kernel_spmd(prog.nc, in_maps[:ncores], core_ids=list(range(ncores)))
    R = list(res.results)
    while len(R) < 8:
        R.append(R[0])
    cat = lambda n: np.concatenate([R[c][n] for c in range(8)], axis=0)
    stk = lambda n: np.stack([R[c][n] for c in range(4)], axis=0)
    y_p = stk("y_p")
    y_s = cat("y_s").reshape(128, 4, D)
    outs = (y_p, y_s,
            stk("conv_p")[None], stk("delta_p")[None], stk("lru_p").reshape(1, 4, 1024), stk("mc_p")[None],
            stk("mn_p")[None], stk("mm_p").reshape(1, 4, 8),
            cat("conv_s")[None], cat("delta_s")[None], cat("lru_s")[None], cat("mc_s")[None],
            cat("mn_s")[None], cat("mm_s")[None])
    return tuple(np.ascontiguousarray(o, dtype=np.float32) for o in outs)
```
